# Optimizing a Trainium2 kernel written in Bass

```python
import math
import jax, jax.numpy as jnp
from jax import lax
import numpy as np

D_MODEL = 1024
BATCH = 8
SEQ = 4096
DEPTH = 4
DEC_BATCH = 16
DEC_SEQ = 16
PAST_LEN = 1024

CHUNK = 64
Q_BLOCK = 128
MIX_WIDTH = D_MODEL
SSM_WIDTH = MIX_WIDTH // 2
SSM_GROUP = 16
SSM_GROUPS = SSM_WIDTH // SSM_GROUP
SSM_STATE = 64
ATT_WIDTH = MIX_WIDTH - SSM_WIDTH
HEAD_DIM = 64
N_HEADS = ATT_WIDTH // HEAD_DIM
IN_WIDTH = 2 * SSM_WIDTH + 4 * ATT_WIDTH
EPS = 1e-6
DT_MIN = 1e-3
DT_MAX = 1e-1

kernel_name = "hymba_s5_stickbreaking_streaming_encoder"


def rms_norm(x, g):
    xf = x.astype(jnp.float32)
    y = xf * lax.rsqrt(jnp.mean(xf * xf, axis=-1, keepdims=True) + EPS)
    return (y * g.astype(jnp.float32)).astype(x.dtype)


def s5_branch(u, h0_re, h0_im, a_re, a_im, log_dt, b_re, b_im, c_re, c_im, d_skip):
    f32 = jnp.float32
    bsz, seq_len, _ = u.shape
    uf = u.astype(f32).reshape(bsz, seq_len, SSM_GROUPS, SSM_GROUP)
    dt = jnp.exp(log_dt.astype(f32))[:, None]
    lr = a_re.astype(f32)
    li = a_im.astype(f32)
    mag = jnp.exp(lr * dt)
    ang = li * dt
    ab_re = mag * jnp.cos(ang)
    ab_im = mag * jnp.sin(ang)
    den = lr * lr + li * li
    nr = ab_re - 1.0
    f_re = (nr * lr + ab_im * li) / den
    f_im = (ab_im * lr - nr * li) / den
    br = b_re.astype(f32)
    bi = b_im.astype(f32)
    bb_re = f_re[..., None] * br - f_im[..., None] * bi
    bb_im = f_re[..., None] * bi + f_im[..., None] * br
    bu_re = jnp.einsum('blgc,gpc->blgp', uf, bb_re)
    bu_im = jnp.einsum('blgc,gpc->blgp', uf, bb_im)
    h0r = h0_re.astype(f32)
    h0i = h0_im.astype(f32)
    bu_re = bu_re.at[:, 0].add(ab_re * h0r - ab_im * h0i)
    bu_im = bu_im.at[:, 0].add(ab_re * h0i + ab_im * h0r)
    a_el_re = jnp.broadcast_to(ab_re, (1, seq_len, SSM_GROUPS, SSM_STATE))
    a_el_im = jnp.broadcast_to(ab_im, (1, seq_len, SSM_GROUPS, SSM_STATE))

    def combine(e1, e2):
        a1r, a1i, b1r, b1i = e1
        a2r, a2i, b2r, b2i = e2
        return (a1r * a2r - a1i * a2i,
                a1r * a2i + a1i * a2r,
                a2r * b1r - a2i * b1i + b2r,
                a2r * b1i + a2i * b1r + b2i)

    _, _, hr, hi = lax.associative_scan(combine, (a_el_re, a_el_im, bu_re, bu_im), axis=1)
    y = (jnp.einsum('blgp,gcp->blgc', hr, c_re.astype(f32))
         - jnp.einsum('blgp,gcp->blgc', hi, c_im.astype(f32)))
    y = y.reshape(bsz, seq_len, SSM_WIDTH) + d_skip.astype(f32) * uf.reshape(bsz, seq_len, SSM_WIDTH)
    return y, hr[:, -1], hi[:, -1]


def stick_breaking(q, k, v, q_pos, k_pos):
    z = jnp.einsum('bqhd,bkhd->bhqk', q, k, preferred_element_type=jnp.float32) * (HEAD_DIM ** -0.5)
    mask = k_pos[None, :] < q_pos[:, None]
    log_1mb = jnp.where(mask, -jax.nn.softplus(z), 0.0)
    after = lax.cumsum(log_1mb, axis=3, reverse=True) - log_1mb
    a = jnp.where(mask, jnp.exp(jax.nn.log_sigmoid(z) + after), 0.0)
    return jnp.einsum('bhqk,bkhd->bqhd', a.astype(v.dtype), v)


def sb_prompt(q, k, v):
    bsz, seq_len = q.shape[0], q.shape[1]
    nb = seq_len // Q_BLOCK
    qb = q.reshape(bsz, nb, Q_BLOCK, N_HEADS, HEAD_DIM).transpose(1, 0, 2, 3, 4)
    pos = jnp.arange(seq_len, dtype=jnp.int32).reshape(nb, Q_BLOCK)
    k_pos = jnp.arange(seq_len, dtype=jnp.int32)
    out = lax.map(lambda a: stick_breaking(a[0], k, v, a[1], k_pos), (qb, pos))
    return out.transpose(1, 0, 2, 3, 4).reshape(bsz, seq_len, N_HEADS, HEAD_DIM)


def sb_sample(q, k, v, k_past, v_past):
    past = k_past.shape[1]
    t_new = q.shape[1]
    k_all = jnp.concatenate([k_past.astype(k.dtype), k], axis=1)
    v_all = jnp.concatenate([v_past.astype(v.dtype), v], axis=1)
    q_pos = past + jnp.arange(t_new, dtype=jnp.int32)
    k_pos = jnp.arange(past + t_new, dtype=jnp.int32)
    return stick_breaking(q, k_all, v_all, q_pos, k_pos)


def hybrid_layer(x, c, h0_re, h0_im, k_past, v_past, norm_g, w_mod, b_mod, w_in,
                 a_re, a_im, log_dt, b_re, b_im, c_re, c_im, d_skip, w_glu, b_glu,
                 q_norm_g, k_norm_g, w_out):
    bsz, seq_len, _ = x.shape
    mod = jax.nn.silu(c) @ w_mod + b_mod
    shift, scale, gate = jnp.split(mod, 3, axis=-1)
    h = rms_norm(x, norm_g) * (1.0 + scale[:, None]) + shift[:, None]
    proj = h @ w_in
    u, z_s, q, k, v, z_a = jnp.split(
        proj, [SSM_WIDTH, 2 * SSM_WIDTH, 2 * SSM_WIDTH + ATT_WIDTH,
               2 * SSM_WIDTH + 2 * ATT_WIDTH, 2 * SSM_WIDTH + 3 * ATT_WIDTH], axis=-1)
    y_s, hr, hi = s5_branch(u, h0_re, h0_im, a_re, a_im, log_dt, b_re, b_im, c_re, c_im, d_skip)
    g_s = jax.nn.gelu(y_s.astype(x.dtype))
    y_s = g_s * jax.nn.sigmoid(g_s @ w_glu + b_glu)
    y_s = y_s * jax.nn.silu(z_s)
    q = rms_norm(q.reshape(bsz, seq_len, N_HEADS, HEAD_DIM), q_norm_g)
    k = rms_norm(k.reshape(bsz, seq_len, N_HEADS, HEAD_DIM), k_norm_g)
    v = v.reshape(bsz, seq_len, N_HEADS, HEAD_DIM)
    if k_past is None:
        o = sb_prompt(q, k, v)
    else:
        o = sb_sample(q, k, v, k_past, v_past)
    y_a = o.reshape(bsz, seq_len, ATT_WIDTH) * jax.nn.silu(z_a)
    mix = jnp.concatenate([y_s, y_a], axis=-1) @ w_out
    x = x + gate[:, None] * mix
    return x, hr, hi, k, v


def setup_inputs(seed: int = 0) -> dict:
    key = jax.random.key(seed)
    ks = jax.random.split(key, 26)
    f32 = jnp.float32

    def nrm(k, shape, s):
        return jax.random.normal(k, shape, f32) * s

    n_idx = jnp.arange(SSM_STATE, dtype=f32)
    return {
        "x_prompt": nrm(ks[0], (BATCH, SEQ, D_MODEL), 1.0),
        "x_sample": nrm(ks[1], (DEC_BATCH, DEC_SEQ, D_MODEL), 1.0),
        "c_prompt": nrm(ks[2], (BATCH, D_MODEL), 1.0),
        "c_sample": nrm(ks[3], (DEC_BATCH, D_MODEL), 1.0),
        "cache_k": nrm(ks[4], (DEPTH, DEC_BATCH, PAST_LEN, N_HEADS, HEAD_DIM), 1.0),
        "cache_v": nrm(ks[5], (DEPTH, DEC_BATCH, PAST_LEN, N_HEADS, HEAD_DIM), 1.0),
        "state_ssm_re": nrm(ks[6], (DEPTH, DEC_BATCH, SSM_GROUPS, SSM_STATE), 0.1),
        "state_ssm_im": nrm(ks[7], (DEPTH, DEC_BATCH, SSM_GROUPS, SSM_STATE), 0.1),
        "norm_g": 1.0 + nrm(ks[8], (DEPTH, D_MODEL), 0.02),
        "w_mod": nrm(ks[9], (DEPTH, D_MODEL, 3 * D_MODEL), 0.5 * D_MODEL ** -0.5),
        "b_mod": nrm(ks[10], (DEPTH, 3 * D_MODEL), 0.01),
        "w_in": nrm(ks[11], (DEPTH, D_MODEL, IN_WIDTH), D_MODEL ** -0.5),
        "ssm_a_re": -0.5 + nrm(ks[12], (DEPTH, SSM_GROUPS, SSM_STATE), 0.01),
        "ssm_a_im": math.pi * n_idx[None, None, :] + nrm(ks[13], (DEPTH, SSM_GROUPS, SSM_STATE), 0.01),
        "ssm_log_dt": jax.random.uniform(ks[14], (DEPTH, SSM_GROUPS), f32,
                                         minval=math.log(DT_MIN), maxval=math.log(DT_MAX)),
        "ssm_b_re": nrm(ks[15], (DEPTH, SSM_GROUPS, SSM_STATE, SSM_GROUP), (2 * SSM_GROUP) ** -0.5),
        "ssm_b_im": nrm(ks[16], (DEPTH, SSM_GROUPS, SSM_STATE, SSM_GROUP), (2 * SSM_GROUP) ** -0.5),
        "ssm_c_re": nrm(ks[17], (DEPTH, SSM_GROUPS, SSM_GROUP, SSM_STATE), SSM_STATE ** -0.5),
        "ssm_c_im": nrm(ks[18], (DEPTH, SSM_GROUPS, SSM_GROUP, SSM_STATE), SSM_STATE ** -0.5),
        "ssm_d": nrm(ks[19], (DEPTH, SSM_WIDTH), 1.0),
        "w_glu": nrm(ks[20], (DEPTH, SSM_WIDTH, SSM_WIDTH), SSM_WIDTH ** -0.5),
        "b_glu": nrm(ks[21], (DEPTH, SSM_WIDTH), 0.01),
        "q_norm_g": 1.0 + nrm(ks[22], (DEPTH, HEAD_DIM), 0.02),
        "k_norm_g": 1.0 + nrm(ks[23], (DEPTH, HEAD_DIM), 0.02),
        "w_out": nrm(ks[24], (DEPTH, MIX_WIDTH, D_MODEL), MIX_WIDTH ** -0.5),
    }


def reference(x_prompt, x_sample, c_prompt, c_sample, cache_k, cache_v, state_ssm_re, state_ssm_im,
              norm_g, w_mod, b_mod, w_in, ssm_a_re, ssm_a_im, ssm_log_dt, ssm_b_re, ssm_b_im,
              ssm_c_re, ssm_c_im, ssm_d, w_glu, b_glu, q_norm_g, k_norm_g, w_out):
    xp = x_prompt
    xs = x_sample
    bsz = x_prompt.shape[0]
    zero_state = jnp.zeros((bsz, SSM_GROUPS, SSM_STATE), jnp.float32)
    pk, pv, pr, pi_, sk, sv, sr, si = [], [], [], [], [], [], [], []
    for l in range(DEPTH):
        lw = (norm_g[l], w_mod[l], b_mod[l], w_in[l], ssm_a_re[l], ssm_a_im[l], ssm_log_dt[l],
              ssm_b_re[l], ssm_b_im[l], ssm_c_re[l], ssm_c_im[l], ssm_d[l], w_glu[l], b_glu[l],
              q_norm_g[l], k_norm_g[l], w_out[l])
        xp, hr, hi, k_new, v_new = hybrid_layer(xp, c_prompt, zero_state, zero_state, None, None, *lw)
        pk.append(k_new); pv.append(v_new); pr.append(hr); pi_.append(hi)
        xs, hr, hi, k_new, v_new = hybrid_layer(xs, c_sample, state_ssm_re[l], state_ssm_im[l],
                                                cache_k[l], cache_v[l], *lw)
        sk.append(k_new); sv.append(v_new); sr.append(hr); si.append(hi)
    return (xp, xs,
            jnp.stack(pk), jnp.stack(pv), jnp.stack(pr), jnp.stack(pi_),
            jnp.stack(sk), jnp.stack(sv), jnp.stack(sr), jnp.stack(si))
```

```python
import contextlib
import math

import ml_dtypes
import numpy as np

import concourse.bass as bass
import concourse.mybir as mybir
from concourse.alu_op_type import AluOpType as ALU
from concourse.bass_utils import run_bass_kernel_spmd

F32 = mybir.dt.float32
BF16 = mybir.dt.bfloat16
AF = mybir.ActivationFunctionType
AX = mybir.AxisListType

D = 1024
DC = 8
NH = 8
HD = 64
G = 32
PST = 64
ATT = 512
SSMW = 512
EPS = 1e-6
NEG = -30000.0
MAGIC = 12582912.0
TWO_PI = 2.0 * math.pi
C1 = 6.28125
C2 = TWO_PI - 6.28125


class Buf:
    __slots__ = ("w", "r")

    def __init__(self):
        self.w = {}
        self.r = {}


class Eng:
    def __init__(self, kern, name, eng, safe):
        self.k = kern
        self.name = name
        self.eng = eng
        self.safe = safe
        self.sem = kern.new_sem(name)
        self.cnt = 0
        self.waited = {}
        self.nsem = 0


class T:
    def __init__(self, h, nbuf=1):
        self.h = h
        self.b = Buf()

    def __getitem__(self, idx):
        return self.h[idx]


class Kern:
    def __init__(self, SEQ=4096, DEPTH=4, NS=2, TS=16, PAST=1024):
        self.SEQ, self.DEPTH, self.NS, self.TS, self.PAST = SEQ, DEPTH, NS, TS, PAST
        self.nc = bass.Bass("TRN2", target_bir_lowering=False)
        self.es = contextlib.ExitStack()
        self.nsem = 0
        nc = self.nc
        self.pe = Eng(self, "pe", nc.tensor, True)
        self.act = Eng(self, "act", nc.scalar, False)
        self.dve = Eng(self, "dve", nc.vector, False)
        self.pool = Eng(self, "pool", nc.gpsimd, False)
        self.sp = Eng(self, "sp", nc.sync, True)
        self.engs = [self.pe, self.act, self.dve, self.pool, self.sp]
        self.dsems = {"sp": [], "pool": []}
        for q, n in (("sp", 24), ("pool", 10)):
            for i in range(n):
                self.dsems[q].append([self.new_sem(f"d{q}{i}"), 0])
        self.dnext = {"sp": 0, "pool": 0}
        self.uid = 0
        self.debug = False
        self.dbg_names = []
        self.rec = None

    def new_sem(self, name):
        self.nsem += 1
        return self.es.enter_context(self.nc.semaphore(f"{name}_{self.nsem}"))

    def _wait(self, E, sem, val):
        if E.waited.get(id(sem), 0) >= val:
            return
        E.eng.wait_ge(sem, val)
        E.waited[id(sem)] = val

    def _deps(self, E, reads, writes):
        for b in reads:
            for sem, val in b.w.values():
                if sem is E.sem and E.safe:
                    continue
                self._wait(E, sem, val)
        for b in writes:
            toks = list(b.r.values()) + list(b.w.values())
            for sem, val in toks:
                if sem is E.sem and E.safe:
                    continue
                self._wait(E, sem, val)

    def _mark(self, tok, reads, writes, is_dma=False):
        for b in reads:
            b.r[id(tok[0])] = tok
        for b in writes:
            if is_dma:
                b.w[id(tok[0])] = tok
            else:
                b.w = {id(tok[0]): tok}
            b.r = {}

    def op(self, E, fn, reads=(), writes=()):
        if self.rec is not None:
            self.rec.append(lambda: self.op(E, fn, reads, writes))
            return
        reads = [x.b if isinstance(x, T) else x for x in reads]
        writes = [x.b if isinstance(x, T) else x for x in writes]
        self._deps(E, reads, writes)
        inst = fn()
        E.cnt += 1
        inst.then_inc(E.sem, 1)
        self._mark((E.sem, E.cnt), reads, writes)
        if E.cnt >= 30000:
            E.sem = self.new_sem(E.name)
            E.cnt = 0

    def dma(self, q, out, in_, reads=(), writes=(), **kw):
        if self.rec is not None:
            self.rec.append(lambda: self.dma(q, out, in_, reads, writes, **kw))
            return
        E = self.sp if q == "sp" else self.pool
        reads = [x.b if isinstance(x, T) else x for x in reads]
        writes = [x.b if isinstance(x, T) else x for x in writes]
        self._deps(E, reads, writes)
        lst = self.dsems[q]
        i = self.dnext[q]
        self.dnext[q] = (i + 1) % len(lst)
        ent = lst[i]
        self._wait(E, ent[0], ent[1])
        inst = E.eng.dma_start(out=out, in_=in_, **kw)
        ent[1] += 16
        inst.then_inc(ent[0], 16)
        self._mark((ent[0], ent[1]), reads, writes, is_dma=True)

    def dbg(self, name, t, ap, shape, dt=F32):
        if not getattr(self, "debug", False):
            return
        if name in self.dbg_names:
            return
        self.dbg_names.append(name)
        d = self.dram("dbg_" + name, list(shape), dt, "ExternalOutput")
        idx = tuple(slice(None) for _ in shape)
        self.dma("sp", d[idx], ap, [t], [d])

    def record(self, fn):
        assert self.rec is None
        self.rec = []
        try:
            fn()
        finally:
            out, self.rec = self.rec, None
        return out

    def barrier(self):
        toks = [(e.sem, e.cnt) for e in self.engs if e.cnt > 0]
        for q in ("sp", "pool"):
            for ent in self.dsems[q]:
                if ent[1] > 0:
                    toks.append((ent[0], ent[1]))
        for e in self.engs:
            for sem, val in toks:
                if sem is e.sem and e.safe:
                    continue
                self._wait(e, sem, val)

    def sb(self, shape, dt, name=None):
        self.uid += 1
        return T(self.es.enter_context(self.nc.sbuf_tensor(f"{name or 't'}_{self.uid}", list(shape), dt)))

    def sb_in(self, stack, shape, dt, name=None):
        self.uid += 1
        return T(stack.enter_context(self.nc.sbuf_tensor(f"{name or 't'}_{self.uid}", list(shape), dt)))

    def dram(self, name, shape, dt, kind):
        return T(self.nc.dram_tensor(name, list(shape), dt, kind=kind).ap())

    def mm(self, out, lhsT, rhs, start, stop, reads, writes):
        self.op(self.pe, lambda: self.nc.tensor.matmul(out, lhsT, rhs, start=start, stop=stop,
                                                       skip_group_check=True), reads, writes)

    def tr(self, out, in_, ident, reads, writes):
        self.op(self.pe, lambda: self.nc.tensor.transpose(out, in_, ident), reads, writes)

    def a(self, out, in_, func, reads, writes, **kw):
        self.op(self.act, lambda: self.nc.scalar.activation(out=out, in_=in_, func=func, **kw), reads, writes)

    def tt(self, out, in0, in1, op, reads, writes, E=None):
        E = E or self.dve
        self.op(E, lambda: E.eng.tensor_tensor(out=out, in0=in0, in1=in1, op=op), reads, writes)

    def ts(self, out, in0, s1, s2, op0, op1, reads, writes, E=None):
        E = E or self.dve
        if op1 is None:
            self.op(E, lambda: E.eng.tensor_scalar(out=out, in0=in0, scalar1=s1, scalar2=None, op0=op0), reads, writes)
        else:
            self.op(E, lambda: E.eng.tensor_scalar(out=out, in0=in0, scalar1=s1, scalar2=s2, op0=op0, op1=op1),
                    reads, writes)

    def stt(self, out, in0, scalar, in1, op0, op1, reads, writes):
        self.op(self.dve, lambda: self.nc.vector.scalar_tensor_tensor(out=out, in0=in0, scalar=scalar, in1=in1,
                                                                     op0=op0, op1=op1), reads, writes)

    def cp(self, out, in_, reads, writes, E=None):
        E = E or self.dve
        if E is self.act:
            self.op(E, lambda: self.nc.scalar.copy(out=out, in_=in_), reads, writes)
        else:
            self.op(E, lambda: E.eng.tensor_copy(out=out, in_=in_), reads, writes)

    def ms(self, ap, val, writes, E=None):
        E = E or self.dve
        self.op(E, lambda: E.eng.memset(ap, val), (), writes)

    def declare_io(self):
        SEQ, L, NS, TS, PAST = self.SEQ, self.DEPTH, self.NS, self.TS, self.PAST
        d = self.dram
        I = "ExternalInput"
        O = "ExternalOutput"
        self.xp = d("xp", [SEQ, D], F32, I)
        self.xs = d("xs", [NS, TS, D], F32, I)
        self.cc = d("cc", [1 + NS, D], F32, I)
        self.ck = d("ck", [L, NS, PAST, ATT], F32, I)
        self.cv = d("cv", [L, NS, PAST, ATT], F32, I)
        self.sre = d("sre", [L, NS, G, PST], F32, I)
        self.sim = d("sim", [L, NS, G, PST], F32, I)
        self.norm_g = d("norm_g", [L, D], F32, I)
        self.w_mod = d("w_mod", [L, D, 3 * D], F32, I)
        self.b_mod = d("b_mod", [L, 3 * D], F32, I)
        self.w_in = d("w_in", [L, D, 3 * D], F32, I)
        self.a_re = d("a_re", [L, G, PST], F32, I)
        self.a_im = d("a_im", [L, G, PST], F32, I)
        self.log_dt = d("log_dt", [L, G], F32, I)
        self.b_re = d("b_re", [L, G, PST, 16], F32, I)
        self.b_im = d("b_im", [L, G, PST, 16], F32, I)
        self.c_re = d("c_re", [L, G, 16, PST], F32, I)
        self.c_im = d("c_im", [L, G, 16, PST], F32, I)
        self.ssm_d = d("ssm_d", [L, SSMW], F32, I)
        self.w_glu = d("w_glu", [L, SSMW, SSMW], F32, I)
        self.b_glu = d("b_glu", [L, SSMW], F32, I)
        self.qg = d("qg", [L, HD], F32, I)
        self.kg = d("kg", [L, HD], F32, I)
        self.w_out = d("w_out", [L, D, D], F32, I)
        self.c_idb = d("c_idb", [128, 128], BF16, I)
        self.c_idf = d("c_idf", [128, 128], F32, I)
        self.c_ntri = d("c_ntri", [128, 128], BF16, I)
        self.c_nones = d("c_nones", [128, 128], BF16, I)
        self.c_negm = d("c_negm", [128, 128], BF16, I)
        self.c_mst = d("c_mst", [128, 128], F32, I)
        self.c_rep = d("c_rep", [16, 128], F32, I)
        self.yp = d("yp", [SEQ, D], F32, O)
        self.ys = d("ys", [NS, TS, D], F32, O)
        self.pk = d("pk", [L, SEQ, ATT], F32, O)
        self.pv = d("pv", [L, SEQ, ATT], F32, O)
        self.pr = d("pr", [L, G, PST], F32, O)
        self.pi = d("pi", [L, G, PST], F32, O)
        self.sk = d("sk", [L, NS, TS, ATT], F32, O)
        self.sv = d("sv", [L, NS, TS, ATT], F32, O)
        self.sr = d("sr", [L, NS, G, PST], F32, O)
        self.si = d("si", [L, NS, G, PST], F32, O)
        self.mixs = d("mixs", [SSMW, SEQ], BF16, "Internal")
        self.mixss = d("mixss", [NS, SSMW, TS], BF16, "Internal")

    def build(self):
        nc = self.nc
        self.declare_io()
        sb = self.sb
        self.ps = []
        for i in range(8):
            self.ps.append(T(self.es.enter_context(nc.psum_tensor(f"psb{i}", [128, 512], F32))))
        self.idb = sb([128, 128], BF16, "idb")
        self.idf = sb([128, 128], F32, "idf")
        self.ntri = sb([128, 128], BF16, "ntri")
        self.nones = sb([128, 128], BF16, "nones")
        self.negm = sb([128, 128], BF16, "negm")
        self.mst = sb([128, 128], F32, "mst")
        self.rep = sb([16, 128], F32, "rep")
        for t, src in ((self.idb, self.c_idb), (self.idf, self.c_idf), (self.ntri, self.c_ntri),
                       (self.nones, self.c_nones), (self.negm, self.c_negm), (self.mst, self.c_mst)):
            self.dma("sp", t[:], src[:, :], [src], [t])
        self.dma("sp", self.rep[:], self.c_rep[:, :], [self.c_rep], [self.rep])
        self.onesf = sb([1, 128], F32, "onesf")
        self.ms(self.onesf[:], 1.0, [self.onesf])
        NR = 1 + self.NS
        self.NR = NR
        self.scT = sb([128, DC, 4], BF16, "scT")
        self.ms(self.scT[:], 0.0, [self.scT])
        with contextlib.ExitStack() as st0:
            crow = self.sb_in(st0, [NR, D], F32, "crow")
            self.dma("sp", crow[:], self.cc[:, :], [self.cc], [crow])
            srow = self.sb_in(st0, [NR, D], F32, "srow")
            self.a(srow[:], crow[:], AF.Silu, [crow], [srow])
            pt = self.ps[0]
            for dc in range(DC):
                self.tr(pt[:, dc * 4:dc * 4 + NR], srow[:, dc * 128:(dc + 1) * 128], self.idf[0:NR, 0:NR], [srow, self.idf], [pt])
            for dc in range(DC):
                self.cp(self.scT[:, dc, 0:NR], pt[:, dc * 4:dc * 4 + NR], [pt], [self.scT])
            self.barrier()
        self.gate_bc = sb([128, NR, D], F32, "gate_bc")
        self.ascale = sb([128, NR, DC], F32, "ascale")
        self.shiftT = sb([128, NR, DC], F32, "shiftT")
        self.normgT = sb([128, DC, 2], F32, "normgT")

        seqs = [dict(name="p", T=self.SEQ, MT=min(512, self.SEQ), r=0, past=0, b=None)]
        for b in range(self.NS):
            seqs.append(dict(name=f"s{b}", T=self.TS, MT=self.TS, r=1 + b, past=self.PAST, b=b))
        self.seqs = seqs

        for l in range(self.DEPTH):
            self.mod_phase(l)
            self.barrier()
            with contextlib.ExitStack() as st:
                self.passA_setup(st, l)
                for sq in seqs:
                    self.passA(st, sq, l)
                self.barrier()
            with contextlib.ExitStack() as st:
                self.passB_setup(st, l)
                for sq in seqs:
                    self.passB(st, sq, l)
                self.barrier()
        self.barrier()
        self.es.close()
        return nc

    def mod_phase(self, l):
        NR = self.NR
        with contextlib.ExitStack() as st:
            wm = self.sb_in(st, [128, DC, D], BF16, "wm")
            self.bmodrow = self.sb_in(st, [1, 3 * D], F32, "bmodrow")
            self.ngrow = self.sb_in(st, [1, D], F32, "ngrow")
            self.screp = self.sb_in(st, [128, NR, DC, 128], BF16, "screp")
            for r in range(NR):
                for dc in range(DC):
                    self.cp(self.screp[:, r, dc, :], self.scT[:, dc, r:r + 1].broadcast_to([128, 128]), [self.scT],
                            [self.screp])
            self.dma("sp", self.bmodrow[:], self.b_mod[l:l + 1, :], [self.b_mod], [self.bmodrow])
            self.dma("sp", self.ngrow[:], self.norm_g[l:l + 1, :], [self.norm_g], [self.ngrow])
            modT = self.sb_in(st, [128, 2, DC, 4], F32, "modT")
            for blk in range(3):
                for h2 in range(2):
                    self.dma("pool", wm[:, h2 * 4:(h2 + 1) * 4, :],
                             self.w_mod[l, h2 * 512:(h2 + 1) * 512, blk * D:(blk + 1) * D].rearrange("(c p) f -> p c f", p=128),
                             [self.w_mod], [wm])
                if blk < 2:
                    pt = self.ps[1]
                    for fc in range(DC):
                        o = pt[:, fc * 4:fc * 4 + 4]
                        for dc in range(DC):
                            self.mm(o, wm[:, dc, fc * 128:(fc + 1) * 128], self.scT[:, dc, 0:4], dc == 0, False,
                                    [wm, self.scT], [pt])
                        self.mm(o, self.bmodrow[0:1, blk * D + fc * 128: blk * D + (fc + 1) * 128],
                                self.onesf[0:1, 0:4], False, True, [self.bmodrow, self.onesf], [pt])
                    self.cp(modT[:, blk, :, :], pt[:, 0:DC * 4].rearrange("p (f r) -> p f r", r=4), [pt], [modT])
                else:
                    for r in range(NR):
                        for hf in range(2):
                            pt = self.ps[2 + hf]
                            for dc in range(DC):
                                self.mm(pt[:, :], self.screp[:, r, dc, :], wm[:, dc, hf * 512:(hf + 1) * 512], dc == 0, False,
                                        [self.screp, wm], [pt])
                            self.mm(pt[:, :], self.onesf[0:1, 0:128],
                                    self.bmodrow[0:1, 2 * D + hf * 512: 2 * D + (hf + 1) * 512], False, True,
                                    [self.onesf, self.bmodrow], [pt])
                            self.cp(self.gate_bc[:, r, hf * 512:(hf + 1) * 512], pt[:, :], [pt], [self.gate_bc])
            pt = self.ps[4]
            for fc in range(DC):
                self.mm(pt[:, fc * 2:fc * 2 + 2], self.ngrow[0:1, fc * 128:(fc + 1) * 128], self.onesf[0:1, 0:2], True, True,
                        [self.ngrow, self.onesf], [pt])
            self.cp(self.normgT[:], pt[:, 0:DC * 2].rearrange("p (f r) -> p f r", r=2), [pt], [self.normgT])
            for r in range(NR):
                self.stt(self.ascale[:, r, :], modT[:, 1, :, r], 1.0, self.normgT[:, :, 0], ALU.add, ALU.mult,
                         [modT, self.normgT], [self.ascale])
                self.cp(self.shiftT[:, r, :], modT[:, 0, :, r], [modT], [self.shiftT])
            self.barrier()

    def norm_tile(self, src_ap, src_t, n, r, xt, xn, hT, col0, small, pbank, light=False):
        self.dma("sp", xt[0:n, :], src_ap, [src_t], [xt])
        ss = small["ss"]
        self.a(xn[0:n, :], xt[0:n, :], AF.Square, [xt], [xn, ss], accum_out=ss[0:n, 0:1])
        self.a(ss[0:n, 1:2], ss[0:n, 0:1], AF.Ln, [ss, small["eps"]], [ss], scale=1.0 / D, bias=small["eps"][0:n, 0:1])
        self.a(ss[0:n, 2:3], ss[0:n, 1:2], AF.Exp, [ss], [ss], scale=-0.5)
        if light:
            self.ts(xn[0:n, :], xt[0:n, :], ss[0:n, 2:3], None, ALU.mult, None, [xt, ss], [xn])
        else:
            self.a(xn[0:n, :], xt[0:n, :], AF.Copy, [xt, ss], [xn], scale=ss[0:n, 2:3])
        pb = pbank[:].bitcast(BF16)
        for dc in range(DC):
            self.tr(pb[:, dc * 128:dc * 128 + n], xn[0:n, dc * 128:(dc + 1) * 128], self.idb[0:n, 0:n],
                    [xn, self.idb], [pbank])
        for dc in range(DC):
            E = self.dve if (dc % 2 == 0 or light) else self.act
            if E is self.dve:
                self.ts(hT[:, dc, col0:col0 + n], pb[:, dc * 128:dc * 128 + n], self.ascale[:, r, dc:dc + 1],
                        self.shiftT[:, r, dc:dc + 1], ALU.mult, ALU.add, [pbank, self.ascale, self.shiftT], [hT])
            else:
                self.a(hT[:, dc, col0:col0 + n], pb[:, dc * 128:dc * 128 + n], AF.Identity,
                       [pbank, self.ascale, self.shiftT], [hT], scale=self.ascale[:, r, dc:dc + 1],
                       bias=self.shiftT[:, r, dc:dc + 1])

    def mk_small(self, st):
        sm = dict(ss=self.sb_in(st, [128, 4], F32, "ss"),
                  eps=self.sb_in(st, [128, 1], F32, "eps"))
        self.ms(sm["eps"][:], EPS, [sm["eps"]])
        return sm

    def x_src(self, sq, l):
        if sq["b"] is None:
            return (self.xp if l == 0 else self.yp), None
        return (self.xs if l == 0 else self.ys), sq["b"]

    def x_rows(self, t, b, r0, n):
        if b is None:
            return t[r0:r0 + n, :]
        return t[b, r0:r0 + n, :]

    def passA_setup(self, st, l):
        sbi = lambda shape, dt, name: self.sb_in(st, shape, dt, name)
        A = {}
        self.A = A
        A["w_in"] = sbi([128, DC, 1024], BF16, "w_inA")
        for h2 in range(2):
            self.dma("pool", A["w_in"][:, h2 * 4:(h2 + 1) * 4, :],
                     self.w_in[l, h2 * 512:(h2 + 1) * 512, 0:1024].rearrange("(c p) f -> p c f", p=128),
                     [self.w_in], [A["w_in"]])
        A["w_glu"] = sbi([128, 4, SSMW], BF16, "w_glu")
        self.dma("pool", A["w_glu"][:], self.w_glu[l].rearrange("(c p) f -> p c f", p=128), [self.w_glu], [A["w_glu"]])
        A["bglu"] = sbi([128, 4], F32, "bglu")
        self.dma("sp", A["bglu"][:], self.b_glu[l].rearrange("(c p) -> p c", p=128), [self.b_glu], [A["bglu"]],
                 allow_slow_non_contiguous=True)
        A["WG"] = sbi([128, 2, G, 128], BF16, "WG")
        A["W0"] = sbi([128, G, 8, 32], BF16, "W0")
        A["WY"] = sbi([128, G, 8, 32], BF16, "WY")
        A["W0b"] = sbi([128, 8, 8, 64], BF16, "W0b")
        A["WYb"] = sbi([128, 8, 8, 64], BF16, "WYb")
        self.ms(A["W0b"][:], 0.0, [A["W0b"]])
        self.ms(A["WYb"][:], 0.0, [A["WYb"]], E=self.pool)
        A["CR"] = sbi([128, 64], F32, "CR")
        A["CI"] = sbi([128, 64], F32, "CI")
        A["CR2"] = sbi([128, 64], F32, "CR2")
        A["CI2"] = sbi([128, 64], F32, "CI2")
        A["q1"] = sbi([128, 8, 64], F32, "q1")
        A["q2"] = sbi([128, 8, 64], F32, "q2")
        A["q3"] = sbi([128, 8, 64], F32, "q3")
        A["q4"] = sbi([128, 8, 64], F32, "q4")
        self.ms(A["W0"][:], 0.0, [A["W0"]])
        self.ms(A["WY"][:], 0.0, [A["WY"]], E=self.pool)
        self.s5_precompute(l, A)
        self.barrier()
        A["xt"] = sbi([128, D], F32, "xtA")
        A["xn"] = sbi([128, D], BF16, "xnA")
        A["hT"] = sbi([128, DC, 512], BF16, "hTA")
        A["XX"] = sbi([64, G, 8, 16], BF16, "XX")
        A["U8"] = [sbi([128, G, 64], BF16, "U8a"), sbi([128, G, 64], BF16, "U8b")]
        A["zs"] = [sbi([128, 4, 512], BF16, "zsa"), sbi([128, 4, 512], BF16, "zsb")]
        A["big"] = [sbi([128, 4096], F32, "bigA"), sbi([128, 4096], F32, "bigB")]
        A["H"] = sbi([128, 65, 64], F32, "H")
        A["t1"] = sbi([128, 64], F32, "t1")
        A["t2"] = sbi([128, 64], F32, "t2")
        A["S0"] = sbi([128, G, 64], BF16, "S0")
        A["gs"] = sbi([128, 4, 512], BF16, "gsA")
        A["small"] = self.mk_small(st)
        A["st32"] = sbi([32, 256], F32, "st32")
        A["fin"] = sbi([32, 128], F32, "fin")

    def s5_precompute(self, l, A):
        with contextlib.ExitStack() as st:
            sbi = lambda shape, dt, name: self.sb_in(st, shape, dt, name)
            V = {}

            def v(name):
                if name not in V:
                    V[name] = sbi([128, G], F32, name)
                return V[name]

            dve_tt = self.tt
            araw = sbi([32, 256], F32, "araw")
            for j in range(2):
                self.dma("sp", araw[:, j * 64:(j + 1) * 64], self.a_re[l], [self.a_re], [araw])
                self.dma("sp", araw[:, 128 + j * 64:128 + (j + 1) * 64], self.a_im[l], [self.a_im], [araw])
            pt = self.ps[0]
            self.tr(pt[:, 0:32], araw[:, 0:128], self.idf[0:32, 0:32], [araw, self.idf], [pt])
            self.tr(pt[:, 32:64], araw[:, 128:256], self.idf[0:32, 0:32], [araw, self.idf], [pt])
            lr, li = v("lr"), v("li")
            self.cp(lr[:], pt[:, 0:32], [pt], [lr])
            self.cp(li[:], pt[:, 32:64], [pt], [li])
            dtb = v("dtb")
            self.dma("sp", dtb[:], self.log_dt[l:l + 1, :].broadcast_to([128, G]), [self.log_dt], [dtb])
            self.a(dtb[:], dtb[:], AF.Exp, [dtb], [dtb])
            rd, mag, ang = v("rd"), v("mag"), v("ang")
            dve_tt(rd[:], lr[:], dtb[:], ALU.mult, [lr, dtb], [rd])
            self.a(mag[:], rd[:], AF.Exp, [rd], [mag])
            dve_tt(ang[:], li[:], dtb[:], ALU.mult, [li, dtb], [ang])
            kq, r_, q_, q2 = v("kq"), v("r_"), v("q_"), v("q2")
            self.ts(kq[:], ang[:], 1.0 / TWO_PI, MAGIC, ALU.mult, ALU.add, [ang], [kq])
            self.ts(kq[:], kq[:], MAGIC, None, ALU.subtract, None, [kq], [kq])
            self.stt(r_[:], kq[:], -C1, ang[:], ALU.mult, ALU.add, [kq, ang], [r_])
            self.stt(r_[:], kq[:], -C2, r_[:], ALU.mult, ALU.add, [kq, r_], [r_])
            self.ts(q_[:], r_[:], 0.25, None, ALU.mult, None, [r_], [q_])
            dve_tt(q2[:], q_[:], q_[:], ALU.mult, [q_], [q2])
            sn, cs, p_ = v("sn"), v("cs"), v("p_")
            self.ts(p_[:], q2[:], 1.0 / 362880.0, None, ALU.mult, None, [q2], [p_])
            for c in (-1.0 / 5040.0, 1.0 / 120.0, -1.0 / 6.0):
                self.stt(p_[:], p_[:], c, q2[:], ALU.add, ALU.mult, [p_, q2], [p_])
            self.stt(sn[:], p_[:], 1.0, q_[:], ALU.add, ALU.mult, [p_, q_], [sn])
            self.ts(p_[:], q2[:], -1.0 / 3628800.0, None, ALU.mult, None, [q2], [p_])
            for c in (1.0 / 40320.0, -1.0 / 720.0, 1.0 / 24.0, -0.5):
                self.stt(p_[:], p_[:], c, q2[:], ALU.add, ALU.mult, [p_, q2], [p_])
            self.ts(cs[:], p_[:], 1.0, None, ALU.add, None, [p_], [cs])
            t_a, t_b = v("t_a"), v("t_b")
            for _ in range(2):
                dve_tt(t_a[:], cs[:], cs[:], ALU.mult, [cs], [t_a])
                dve_tt(t_b[:], sn[:], sn[:], ALU.mult, [sn], [t_b])
                self.stt(sn[:], cs[:], 2.0, sn[:], ALU.mult, ALU.mult, [cs, sn], [sn])
                dve_tt(cs[:], t_a[:], t_b[:], ALU.subtract, [t_a, t_b], [cs])
            PW = sbi([128, 2, 9, G], F32, "PW")
            QW = sbi([128, 2, 8, G], F32, "QW")
            abr, abi = PW[:, 0, 1, :], PW[:, 1, 1, :]
            self.ms(PW[:, 0, 0, :], 1.0, [PW])
            self.ms(PW[:, 1, 0, :], 0.0, [PW])
            dve_tt(abr, mag[:], cs[:], ALU.mult, [mag, cs], [PW])
            dve_tt(abi, mag[:], sn[:], ALU.mult, [mag, sn], [PW])

            def cmul(dst, dr, di, src, ar, ai, br, bi, extra):
                dve_tt(t_a[:], ar, br, ALU.mult, [src] + extra, [t_a])
                dve_tt(t_b[:], ai, bi, ALU.mult, [src] + extra, [t_b])
                dve_tt(dr, t_a[:], t_b[:], ALU.subtract, [t_a, t_b], [dst])
                dve_tt(t_a[:], ar, bi, ALU.mult, [src] + extra, [t_a])
                dve_tt(t_b[:], ai, br, ALU.mult, [src] + extra, [t_b])
                dve_tt(di, t_a[:], t_b[:], ALU.add, [t_a, t_b], [dst])

            for k in range(1, 8):
                cmul(PW, PW[:, 0, k + 1, :], PW[:, 1, k + 1, :], PW, PW[:, 0, k, :], PW[:, 1, k, :], abr, abi, [])
            m2, ivr, ivi = v("m2"), v("ivr"), v("ivi")
            dve_tt(m2[:], mag[:], mag[:], ALU.mult, [mag], [m2])
            self.op(self.dve, lambda: self.nc.vector.reciprocal(out=m2[:], in_=m2[:]), [m2.b], [m2.b])
            dve_tt(ivr[:], abr, m2[:], ALU.mult, [PW, m2], [ivr])
            self.stt(ivi[:], abi, -1.0, m2[:], ALU.mult, ALU.mult, [PW, m2], [ivi])
            self.ms(QW[:, 0, 0, :], 1.0, [QW])
            self.ms(QW[:, 1, 0, :], 0.0, [QW])
            self.cp(QW[:, 0, 1, :], ivr[:], [ivr], [QW])
            self.cp(QW[:, 1, 1, :], ivi[:], [ivi], [QW])
            for k in range(1, 7):
                cmul(QW, QW[:, 0, k + 1, :], QW[:, 1, k + 1, :], QW, QW[:, 0, k, :], QW[:, 1, k, :], ivr[:], ivi[:],
                     [ivr, ivi])
            self.dbg("PW", PW, PW[:], [128, 2, 9, G])
            self.dbg("QW", QW, QW[:], [128, 2, 8, G])
            self.dbg("cs", cs, cs[:], [128, G])
            self.dbg("sn", sn, sn[:], [128, G])
            self.dbg("mag", mag, mag[:], [128, G])
            self.dbg("ang", ang, ang[:], [128, G])
            self.cp(A["CR"][:, 0:32], PW[:, 0, 8, :], [PW], [A["CR"]])
            self.cp(A["CR"][:, 32:64], PW[:, 0, 8, :], [PW], [A["CR"]])
            self.cp(A["CI"][:, 0:32], PW[:, 1, 8, :], [PW], [A["CI"]])
            self.ts(A["CI"][:, 32:64], PW[:, 1, 8, :], -1.0, None, ALU.mult, None, [PW], [A["CI"]])
            p16r, p16i = v("p16r"), v("p16i")
            cmul(p16r, p16r[:], p16i[:], PW, PW[:, 0, 8, :], PW[:, 1, 8, :], PW[:, 0, 8, :], PW[:, 1, 8, :], [])
            self.cp(A["CR2"][:, 0:32], p16r[:], [p16r], [A["CR2"]])
            self.cp(A["CR2"][:, 32:64], p16r[:], [p16r], [A["CR2"]])
            self.cp(A["CI2"][:, 0:32], p16i[:], [p16i], [A["CI2"]])
            self.ts(A["CI2"][:, 32:64], p16i[:], -1.0, None, ALU.mult, None, [p16i], [A["CI2"]])
            den, nr, fr, fi = v("den"), v("nr"), v("fr"), v("fi")
            dve_tt(t_a[:], lr[:], lr[:], ALU.mult, [lr], [t_a])
            dve_tt(t_b[:], li[:], li[:], ALU.mult, [li], [t_b])
            dve_tt(den[:], t_a[:], t_b[:], ALU.add, [t_a, t_b], [den])
            self.op(self.dve, lambda: self.nc.vector.reciprocal(out=den[:], in_=den[:]), [den.b], [den.b])
            self.ts(nr[:], abr, -1.0, None, ALU.add, None, [PW], [nr])
            dve_tt(t_a[:], nr[:], lr[:], ALU.mult, [nr, lr], [t_a])
            dve_tt(t_b[:], abi, li[:], ALU.mult, [PW, li], [t_b])
            dve_tt(t_a[:], t_a[:], t_b[:], ALU.add, [t_a, t_b], [t_a])
            dve_tt(fr[:], t_a[:], den[:], ALU.mult, [t_a, den], [fr])
            dve_tt(t_a[:], abi, lr[:], ALU.mult, [PW, lr], [t_a])
            dve_tt(t_b[:], nr[:], li[:], ALU.mult, [nr, li], [t_b])
            dve_tt(t_a[:], t_a[:], t_b[:], ALU.subtract, [t_a, t_b], [t_a])
            dve_tt(fi[:], t_a[:], den[:], ALU.mult, [t_a, den], [fi])
            Bs = sbi([128, G, 16], F32, "Bs")
            Br = sbi([128, G, 16], F32, "Br")
            bre = self.b_re[l].rearrange("g p c -> p g c")
            bim = self.b_im[l].rearrange("g p c -> p g c")
            self.dma("sp", Bs[0:64, :, :], bre, [self.b_re], [Bs])
            self.dma("sp", Bs[64:128, :, :], bim, [self.b_im], [Bs])
            self.dma("sp", Br[0:64, :, :], bim, [self.b_im], [Br])
            self.dma("sp", Br[64:128, :, :], bre, [self.b_re], [Br])
            self.ts(Br[0:64, :, :], Br[0:64, :, :], -1.0, None, ALU.mult, None, [Br], [Br])
            Bbs = sbi([128, G, 16], F32, "Bbs")
            Bbr = sbi([128, G, 16], F32, "Bbr")
            tm1 = sbi([128, G, 16], F32, "tm1")
            tm2 = sbi([128, G, 16], F32, "tm2")

            def bc(ap2):
                return ap2.unsqueeze(2).broadcast_to([128, G, 16])

            dve_tt(tm1[:], Bs[:], bc(fr[:]), ALU.mult, [Bs, fr], [tm1])
            dve_tt(tm2[:], Br[:], bc(fi[:]), ALU.mult, [Br, fi], [tm2])
            dve_tt(Bbs[:], tm1[:], tm2[:], ALU.add, [tm1, tm2], [Bbs])
            dve_tt(tm1[:], Br[:], bc(fr[:]), ALU.mult, [Br, fr], [tm1])
            dve_tt(tm2[:], Bs[:], bc(fi[:]), ALU.mult, [Bs, fi], [tm2])
            dve_tt(Bbr[:], tm1[:], tm2[:], ALU.subtract, [tm1, tm2], [Bbr])
            scrA = sbi([128, G, 8, 16], F32, "scrA")
            scrB = sbi([128, G, 8, 16], F32, "scrB")
            for s in range(8):
                pr_, pi_ = PW[:, 0, 7 - s, :], PW[:, 1, 7 - s, :]
                dve_tt(tm1[:], Bbs[:], bc(pr_), ALU.mult, [Bbs, PW], [tm1])
                dve_tt(tm2[:], Bbr[:], bc(pi_), ALU.mult, [Bbr, PW], [tm2])
                dve_tt(scrA[:, :, s, :], tm1[:], tm2[:], ALU.add, [tm1, tm2], [scrA])
                dve_tt(tm1[:], Bbr[:], bc(pr_), ALU.mult, [Bbr, PW], [tm1])
                dve_tt(tm2[:], Bbs[:], bc(pi_), ALU.mult, [Bbs, PW], [tm2])
                dve_tt(scrB[:, :, s, :], tm1[:], tm2[:], ALU.subtract, [tm1, tm2], [scrB])
            for vi, scr in enumerate((scrA, scrB)):
                for g0 in range(0, G, 4):
                    pt = self.ps[(g0 // 4) % 4]
                    for j in range(4):
                        self.tr(pt[:, j * 128:(j + 1) * 128], scr[:, g0 + j, :, :].rearrange("p s c -> p (s c)"), self.idf[:, :], [scr, self.idf], [pt])
                    E = self.act if (g0 // 4) % 2 else self.dve
                    self.cp(A["WG"][:, vi, g0:g0 + 4, :], pt[:, :].rearrange("p (g f) -> p g f", g=4), [pt], [A["WG"]], E=E)
            for s in range(8):
                qr_, qi_ = QW[:, 0, s, :], QW[:, 1, s, :]
                dve_tt(tm1[:], Bbs[:], bc(qr_), ALU.mult, [Bbs, QW], [tm1])
                dve_tt(tm2[:], Bbr[:], bc(qi_), ALU.mult, [Bbr, QW], [tm2])
                dve_tt(scrA[:, :, s, :], tm1[:], tm2[:], ALU.add, [tm1, tm2], [scrA])
            craw = sbi([128, 4, 256], F32, "craw")
            cre = self.c_re[l].rearrange("(b g) c p -> (g c) b p", b=4)
            cim = self.c_im[l].rearrange("(b g) c p -> (g c) b p", b=4)
            self.dma("sp", craw[:, :, 0:64], cre, [self.c_re], [craw])
            self.dma("sp", craw[:, :, 64:128], cim, [self.c_im], [craw])
            self.dma("sp", craw[:, :, 128:192], cim, [self.c_im], [craw])
            self.dma("sp", craw[:, :, 192:256], cre, [self.c_re], [craw])
            T1 = sbi([128, G, 16], F32, "T1")
            T2 = sbi([128, G, 16], F32, "T2")
            pt = self.ps[4]
            pt2 = self.ps[5]
            for b in range(4):
                self.tr(pt[:, b * 128:(b + 1) * 128], craw[:, b, 0:128], self.idf[:, :], [craw, self.idf], [pt])
                self.tr(pt2[:, b * 128:(b + 1) * 128], craw[:, b, 128:256], self.idf[:, :], [craw, self.idf], [pt2])
            self.cp(T1[:].rearrange("p g c -> p (g c)"), pt[:, :], [pt], [T1])
            self.ts(T1[64:128, :, :], T1[64:128, :, :], -1.0, None, ALU.mult, None, [T1], [T1])
            self.ts(T2[:].rearrange("p g c -> p (g c)"), pt2[:, :], -1.0, None, ALU.mult, None, [pt2], [T2])
            for t in range(9):
                pr_, pi_ = PW[:, 0, t, :], PW[:, 1, t, :]
                dve_tt(tm1[:], T1[:], bc(pr_), ALU.mult, [T1, PW], [tm1])
                dve_tt(tm2[:], T2[:], bc(pi_), ALU.mult, [T2, PW], [tm2])
                if t < 8:
                    dve_tt(scrB[:, :, t, :], tm1[:], tm2[:], ALU.add, [tm1, tm2], [scrB])
                if t >= 1:
                    for par in range(2):
                        dve_tt(A["WY"][:, par:G:2, t - 1, par * 16:par * 16 + 16], tm1[:, par:G:2, :], tm2[:, par:G:2, :],
                               ALU.add, [tm1, tm2], [A["WY"]])
                        dve_tt(A["WYb"][:, par:8:2, t - 1, 32 + par * 16:32 + par * 16 + 16], tm1[:, 6 + par:G:8, :],
                               tm2[:, 6 + par:G:8, :], ALU.add, [tm1, tm2], [A["WYb"]])
            self.dbg("fr", fr, fr[:], [128, G])
            self.dbg("fi", fi, fi[:], [128, G])
            self.dbg("Bbs", Bbs, Bbs[:], [128, G, 16])
            self.dbg("T1", T1, T1[:], [128, G, 16])
            self.dbg("T2", T2, T2[:], [128, G, 16])
            self.dbg("WG", A["WG"], A["WG"][:], [128, 2, G, 128], BF16)
            self.dbg("WY", A["WY"], A["WY"][:], [128, G, 8, 32], BF16)
            dm = sbi([16, G], F32, "dm")
            self.dma("sp", dm[:], self.ssm_d[l].rearrange("(g c) -> c g", c=16), [self.ssm_d], [dm],
                     allow_slow_non_contiguous=True)
            dcol = sbi([128, G], F32, "dcol")
            pt = self.ps[6]
            self.mm(pt[:, 0:G], self.rep[:, :], dm[:, :], True, True, [self.rep, dm], [pt])
            self.cp(dcol[:], pt[:, 0:G], [pt], [dcol])
            w0t = sbi([128, 4, 128], F32, "w0t")
            for g0 in range(0, G, 4):
                pt = self.ps[(g0 // 4) % 4]
                for j in range(4):
                    g = g0 + j
                    self.mm(pt[:, j * 128:(j + 1) * 128], scrA[:, g, :, :].rearrange("p s c -> p (s c)"), scrB[:, g, :, :].rearrange("p s c -> p (s c)"), True, True, [scrA, scrB], [pt])
                self.tt(w0t[:], pt[:, :].rearrange("p (g f) -> p g f", g=4),
                        self.mst[:, :].unsqueeze(1).broadcast_to([128, 4, 128]), ALU.mult, [pt, self.mst], [w0t])
                for j in range(4):
                    g = g0 + j
                    off = (g % 2) * 16
                    if g % 8 >= 6:
                        gi = (g // 8) * 2 + (g % 2)
                        wdst, wt = A["W0b"][:, gi, :, 32 + off:32 + off + 16], A["W0b"]
                    else:
                        wdst, wt = A["W0"][:, g, :, off:off + 16], A["W0"]
                    self.stt(wdst, self.idf[:, :].rearrange("p (t c) -> p t c", c=16),
                             dcol[:, g:g + 1], w0t[:, j, :].rearrange("p (t c) -> p t c", c=16), ALU.mult, ALU.add,
                             [self.idf, dcol, w0t], [wt])

    def passA(self, st, sq, l):
        A = self.A
        self.dbg("W0", A["W0"], A["W0"][:], [128, G, 8, 32], BF16)
        self.dbg("W0b", A["W0b"], A["W0b"][:], [128, 8, 8, 64], BF16)
        Ttot, MT, r = sq["T"], sq["MT"], sq["r"]
        b = sq["b"]
        nm = MT // 8
        nsub = (MT + 127) // 128
        nmt = Ttot // MT
        xsrc, xb = self.x_src(sq, l)
        H = A["H"]
        RE = self.pool
        if b is None:
            self.ms(H[:, 0, :], 0.0, [H], E=RE)
        else:
            st32 = A["st32"]
            self.dma("sp", st32[:, 0:64], self.sre[l, b], [self.sre], [st32])
            self.dma("sp", st32[:, 64:128], self.sim[l, b], [self.sim], [st32])
            self.dma("sp", st32[:, 128:192], self.sim[l, b], [self.sim], [st32])
            self.dma("sp", st32[:, 192:256], self.sre[l, b], [self.sre], [st32])
            pt = self.ps[7]
            self.tr(pt[:, 0:32], st32[:, 0:128], self.idf[0:32, 0:32], [st32, self.idf], [pt])
            self.tr(pt[:, 32:64], st32[:, 128:256], self.idf[0:32, 0:32], [st32, self.idf], [pt])
            self.cp(H[:, 0, :], pt[:, 0:64], [pt], [H])
            self.ts(H[0:64, 0, 32:64], H[0:64, 0, 32:64], -1.0, None, ALU.mult, None, [H], [H])

        def bufs(mt):
            big = A["big"][mt % 2]
            return big, big[:, :].rearrange("p (m v g) -> p m v g", v=2, g=G), A["U8"][mt % 2], A["zs"][mt % 2]

        def X(mt):
            tok0 = mt * MT
            hT = A["hT"]
            big, Gsv, U8, zs = bufs(mt)
            for s_ in range(nsub):
                n = min(128, MT - s_ * 128)
                self.norm_tile(self.x_rows(xsrc, xb, tok0 + s_ * 128, n), xsrc, n, r, A["xt"], A["xn"], hT, s_ * 128,
                               A["small"], self.ps[0])
            XX = A["XX"]
            for s in range(8):
                pt = self.ps[1 + (s % 2)]
                for dc in range(DC):
                    self.mm(pt[0:nm, :], hT[:, dc, s:MT:8], A["w_in"][:, dc, 0:512], dc == 0, dc == DC - 1,
                            [hT, A["w_in"]], [pt])
                self.cp(XX[0:nm, :, s, :], pt[0:nm, :].rearrange("p (g c) -> p g c", c=16), [pt], [XX],
                        E=(self.act if s % 2 else self.dve))
            for g0 in range(0, G, 8):
                pt = self.ps[1 + (g0 // 8) % 2]
                pb = pt[:].bitcast(BF16)
                for j in range(8):
                    g = g0 + j
                    self.tr(pb[:, j * 64:j * 64 + nm], XX[0:nm, g, :, :].rearrange("p s c -> p (s c)"), self.idb[0:nm, 0:nm],
                            [XX, self.idb], [pt])
                self.cp(U8[:, g0:g0 + 8, 0:nm], pb[:, 0:512].rearrange("p (g m) -> p g m", g=8)[:, :, 0:nm], [pt], [U8])
            for fc in range(4):
                pt = self.ps[1 + (fc % 2)]
                for dc in range(DC):
                    self.mm(pt[:, 0:MT], A["w_in"][:, dc, 512 + fc * 128:512 + (fc + 1) * 128], hT[:, dc, 0:MT], dc == 0,
                            dc == DC - 1, [A["w_in"], hT], [pt])
                self.a(zs[:, fc, 0:MT], pt[:, 0:MT], AF.Silu, [pt], [zs])
            for g0 in range(0, G, 4):
                pt = self.ps[1 + (g0 // 4) % 2]
                for vi in range(2):
                    for j in range(4):
                        g = g0 + j
                        c0 = (vi * 4 + j) * 64
                        self.mm(pt[:, c0:c0 + nm], A["WG"][:, vi, g, :], U8[:, g, 0:nm], True, True, [A["WG"], U8], [pt])
                for vi in range(2):
                    self.cp(Gsv[:, 0:nm, vi, g0:g0 + 4].rearrange("p m g -> p g m"),
                            pt[:, vi * 256:(vi + 1) * 256].rearrange("p (g m) -> p g m", g=4)[:, :, 0:nm], [pt], [big],
                            E=(self.act if vi else self.dve))

        def capply(E, dst, src, addend, CRt, CIt, kq, ta, tb, rd, wr):
            crb = CRt[:, :].unsqueeze(1).broadcast_to([128, kq, 64])
            cia = CIt[:, 0:32].unsqueeze(1).broadcast_to([128, kq, 32])
            cib = CIt[:, 32:64].unsqueeze(1).broadcast_to([128, kq, 32])
            self.tt(ta[:, 0:kq, :], src, crb, ALU.mult, rd + [CRt], [ta], E=E)
            self.tt(tb[:, 0:kq, 0:32], src[:, :, 32:64], cia, ALU.mult, rd + [CIt], [tb], E=E)
            self.tt(tb[:, 0:kq, 32:64], src[:, :, 0:32], cib, ALU.mult, rd + [CIt], [tb], E=E)
            self.tt(ta[:, 0:kq, :], ta[:, 0:kq, :], tb[:, 0:kq, :], ALU.add, [ta, tb], [ta], E=E)
            self.tt(dst, ta[:, 0:kq, :], addend, ALU.add, [ta] + rd, wr, E=E)

        def B2(mt):
            big, Gsv, U8, zs = bufs(mt)
            Gf = Gsv.rearrange("p m v g -> p m (v g)")
            nk2 = nm // 2
            for k0 in range(0, nk2, 8):
                kq = min(8, nk2 - k0)
                ev = Gf[:, 2 * k0:2 * (k0 + kq):2, :]
                od = Gf[:, 2 * k0 + 1:2 * (k0 + kq):2, :]
                capply(self.dve, od, ev, od, A["CR"], A["CI"], kq, A["q1"], A["q2"], [big], [big])

        def Rc(mt):
            big, Gsv, U8, zs = bufs(mt)
            Gf = Gsv.rearrange("p m v g -> p m (v g)")
            t1, t2 = A["t1"], A["t2"]
            nk2 = nm // 2
            if mt > 0:
                self.cp(H[:, 0, :], H[:, nm, :], [H], [H], E=RE)
            for k in range(nk2):
                m = 2 * k
                self.tt(t1[:], H[:, m, :], A["CR2"][:], ALU.mult, [H, A["CR2"]], [t1], E=RE)
                self.tt(t2[:, 0:32], H[:, m, 32:64], A["CI2"][:, 0:32], ALU.mult, [H, A["CI2"]], [t2], E=RE)
                self.tt(t2[:, 32:64], H[:, m, 0:32], A["CI2"][:, 32:64], ALU.mult, [H, A["CI2"]], [t2], E=RE)
                self.tt(t1[:], t1[:], t2[:], ALU.add, [t1, t2], [t1], E=RE)
                self.tt(H[:, m + 2, :], t1[:], Gf[:, m + 1, :], ALU.add, [t1, big], [H], E=RE)
            for k0 in range(0, nk2, 8):
                kq = min(8, nk2 - k0)
                hev = H[:, 2 * k0:2 * (k0 + kq):2, :]
                hod = H[:, 2 * k0 + 1:2 * (k0 + kq):2, :]
                gev = Gf[:, 2 * k0:2 * (k0 + kq):2, :]
                capply(RE, hod, hev, gev, A["CR"], A["CI"], kq, A["q3"], A["q4"], [H, big], [H])
            self.cp(A["S0"][:, :, 0:nm], H[:, 0:nm, 0:32].rearrange("p m g -> p g m"), [H], [A["S0"]], E=RE)

        def Y(mt):
            tok0 = mt * MT
            big, Gsv, U8, zs = bufs(mt)
            S0 = A["S0"]
            for fc in range(4):
                pt = self.ps[3 + fc]
                for t in range(8):
                    for pp in range(2):
                        o = pt[32 * pp:32 * pp + 32, t:MT:8]
                        for gi, g in enumerate((8 * fc + 2 * pp, 8 * fc + 2 * pp + 1)):
                            self.mm(o, A["W0"][:, g, t, :], U8[:, g, 0:nm], gi == 0, False, [A["W0"], U8], [pt])
                            self.mm(o, A["WY"][:, g, t, :], S0[:, g, 0:nm], False, gi == 1, [A["WY"], S0], [pt])
                    o64 = pt[64:128, t:MT:8]
                    o32 = pt[64:96, t:MT:8]
                    for gi in range(2):
                        g = 8 * fc + 6 + gi
                        self.mm(o64, A["W0b"][:, 2 * fc + gi, t, :], U8[:, g, 0:nm], gi == 0, False, [A["W0b"], U8], [pt])
                        self.mm(o64, A["WYb"][:, 2 * fc + gi, t, :], S0[:, g, 0:nm], False, False, [A["WYb"], S0], [pt])
                    for gi in range(2):
                        g = 8 * fc + 4 + gi
                        self.mm(o32, A["W0"][:, g, t, :], U8[:, g, 0:nm], False, False, [A["W0"], U8], [pt])
                        self.mm(o32, A["WY"][:, g, t, :], S0[:, g, 0:nm], False, gi == 1, [A["WY"], S0], [pt])
            gs = A["gs"]
            mo = gs
            yy = big[:, 0:2048].rearrange("p (c t) -> p c t", c=4)
            uu = big[:, 2048:4096].rearrange("p (c t) -> p c t", c=4)
            for fc in range(4):
                self.cp(yy[:, fc, 0:MT], self.ps[3 + fc][:, 0:MT], [self.ps[3 + fc]], [big], E=self.act)
            yv, uv = yy[:, :, 0:MT], uu[:, :, 0:MT]
            self.tt(uv, yv, yv, ALU.mult, [big], [big])
            self.ts(uv, uv, 0.044715, 1.0, ALU.mult, ALU.add, [big], [big])
            self.tt(uv, uv, yv, ALU.mult, [big], [big])
            self.a(uv, uv, AF.Sigmoid, [big], [big], scale=2.0 * math.sqrt(2.0 / math.pi))
            self.tt(yv, yv, uv, ALU.mult, [big], [big])
            self.cp(gs[:, :, 0:MT], yv, [big], [gs])
            for fc in range(4):
                pt = self.ps[1 + fc % 2]
                for kc in range(4):
                    self.mm(pt[:, 0:MT], A["w_glu"][:, kc, fc * 128:(fc + 1) * 128], gs[:, kc, 0:MT], kc == 0, kc == 3,
                            [A["w_glu"], gs], [pt])
                self.a(uu[:, fc, 0:MT], pt[:, 0:MT], AF.Sigmoid, [pt, A["bglu"]], [big], bias=A["bglu"][:, fc:fc + 1])
            self.tt(yv, yv, uv, ALU.mult, [big], [big])
            self.tt(mo[:, :, 0:MT], yv, zs[:, :, 0:MT], ALU.mult, [big, zs], [mo])
            if b is None:
                self.dma("sp", self.mixs[:, tok0:tok0 + MT].rearrange("(c p) t -> p c t", p=128), mo[:, :, 0:MT],
                         [mo], [self.mixs])
            else:
                self.dma("sp", self.mixss[b].rearrange("(c p) t -> p c t", p=128), mo[:, :, 0:MT], [mo], [self.mixss])

        X(0)
        B2(0)
        Rc(0)
        for mt in range(nmt):
            if mt + 1 < nmt:
                X(mt + 1)
                B2(mt + 1)
            Y(mt)
            if mt + 1 < nmt:
                Rc(mt + 1)
        fin = A["fin"]
        pt = self.ps[7]
        fsrc = A["t1"]
        self.cp(fsrc[:, 0:32], H[:, nm, 0:32], [H], [fsrc], E=RE)
        self.tr(pt[0:32, 0:128], fsrc[:, 0:32], self.idf[:, :], [fsrc, self.idf], [pt])
        self.cp(fin[:, :], pt[0:32, 0:128], [pt], [fin])
        if b is None:
            self.dma("sp", self.pr[l], fin[:, 0:64], [fin], [self.pr])
            self.dma("sp", self.pi[l], fin[:, 64:128], [fin], [self.pi])
        else:
            self.dma("sp", self.sr[l, b], fin[:, 0:64], [fin], [self.sr])
            self.dma("sp", self.si[l, b], fin[:, 64:128], [fin], [self.si])

    def passB_setup(self, st, l):
        sbi = lambda shape, dt, name: self.sb_in(st, shape, dt, name)
        B = {}
        self.B = B
        B["w_in"] = sbi([128, DC, 2048], BF16, "w_inB")
        for h4 in range(4):
            self.dma("pool", B["w_in"][:, h4 * 2:(h4 + 1) * 2, :],
                     self.w_in[l, h4 * 256:(h4 + 1) * 256, 1024:3072].rearrange("(c p) f -> p c f", p=128),
                     [self.w_in], [B["w_in"]])
        B["w_out"] = sbi([128, DC, D], BF16, "w_out")
        for h2 in range(2):
            self.dma("pool", B["w_out"][:, h2 * 4:(h2 + 1) * 4, :],
                     self.w_out[l, h2 * 512:(h2 + 1) * 512, :].rearrange("(c p) f -> p c f", p=128),
                     [self.w_out], [B["w_out"]])
        KW = max(self.SEQ, self.PAST + self.TS)
        NVB = max(self.SEQ // 128, self.PAST // 128 + 1)
        B["KT"] = sbi([128, 4, KW], BF16, "KT")
        B["V"] = sbi([128, NVB, ATT], BF16, "Vres")
        B["gq"] = sbi([128, HD], F32, "gq")
        B["gk"] = sbi([128, HD], F32, "gk")
        self.dma("sp", B["gq"][:], self.qg[l:l + 1, :].broadcast_to([128, HD]), [self.qg], [B["gq"]])
        self.dma("sp", B["gk"][:], self.kg[l:l + 1, :].broadcast_to([128, HD]), [self.kg], [B["gk"]])
        self.ts(B["gq"][:], B["gq"][:], HD ** -0.5, None, ALU.mult, None, [B["gq"]], [B["gq"]])
        B["xt"] = sbi([128, D], F32, "xtB")
        B["xr"] = sbi([128, D], F32, "xrB")
        B["ot"] = [sbi([128, 512], F32, "ot0"), sbi([128, 512], F32, "ot1")]
        B["xn"] = sbi([128, D], BF16, "xnB")
        B["hT"] = sbi([128, DC, 512], BF16, "hTB")
        B["small"] = self.mk_small(st)
        B["sqk"] = sbi([128, ATT], F32, "sqk")
        B["ssq"] = sbi([128, 16], F32, "ssq")
        B["qf"] = sbi([128, ATT], F32, "qf")
        B["kf"] = sbi([128, ATT], F32, "kf")
        B["vf"] = sbi([128, ATT], F32, "vf")
        B["qb"] = sbi([128, ATT], BF16, "qb")
        B["kb"] = sbi([128, ATT], BF16, "kb")
        B["QT"] = [sbi([128, 4, 512], BF16, "QT0"), sbi([128, 4, 512], BF16, "QT1")]
        B["za"] = [sbi([128, 4, 512], BF16, "za0"), sbi([128, 4, 512], BF16, "za1")]
        B["mix"] = [sbi([128, DC, 512], BF16, "mixT0"), sbi([128, DC, 512], BF16, "mixT1")]
        B["e"] = [sbi([128, 512], F32, "e0"), sbi([128, 512], F32, "e1")]
        B["sp"] = [sbi([128, 512], BF16, "sp0"), sbi([128, 512], BF16, "sp1")]
        B["at"] = [sbi([128, 512], BF16, "at0"), sbi([128, 512], BF16, "at1")]
        B["R"] = [sbi([128, 512], BF16, "R0"), sbi([128, 512], BF16, "R1")]

    def treduce(self, out, in_, reads, writes):
        self.op(self.dve, lambda: self.nc.vector.tensor_reduce(out=out, in_=in_, axis=AX.X, op=ALU.add), reads, writes)

    def passB(self, st, sq, l):
        B = self.B
        Ttot, MT, r, b, past = sq["T"], sq["MT"], sq["r"], sq["b"], sq["past"]
        nsub = (MT + 127) // 128
        nmt = Ttot // MT
        xsrc, xb = self.x_src(sq, l)
        xdst = self.yp if b is None else self.ys
        KT, V = B["KT"], B["V"]
        kvb = [Buf() for _ in range(nmt)]
        pastb = Buf()
        if past:
            npb = past // 128
            self.dma("pool", V[:, 0:npb, :], self.cv[l, b].rearrange("(n p) f -> p n f", p=128), [self.cv], [pastb])
            for n0 in range(npb):
                kb = B["kb"]
                self.dma("pool", kb[:, :], self.ck[l, b, n0 * 128:(n0 + 1) * 128, :], [self.ck], [kb])
                pt = self.ps[n0 % 2]
                pb = pt[:].bitcast(BF16)
                for c in range(4):
                    self.tr(pb[:, c * 128:(c + 1) * 128], kb[:, c * 128:(c + 1) * 128], self.idb[:, :], [kb, self.idb], [pt])
                self.cp(KT[:, :, n0 * 128:(n0 + 1) * 128], pb[:, 0:512].rearrange("p (c t) -> p c t", c=4), [pt], [pastb],
                        E=(self.act if n0 % 2 else self.dve))

        def fe(mt):
            tok0 = mt * MT
            hT = B["hT"]
            QT, za = B["QT"][mt % 2], B["za"][mt % 2]
            for s_ in range(nsub):
                n = min(128, MT - s_ * 128)
                self.norm_tile(self.x_rows(xsrc, xb, tok0 + s_ * 128, n), xsrc, n, r, B["xt"], B["xn"], hT, s_ * 128,
                               B["small"], self.ps[0], light=True)
            for s_ in range(nsub):
                n = min(128, MT - s_ * 128)
                g0 = past + tok0 + s_ * 128
                pq, pk_, pv_ = self.ps[0], self.ps[1], self.ps[0]
                sqk, ssq = B["sqk"], B["ssq"]
                for which, pt, gt, of, ob in ((0, pq, B["gq"], B["qf"], B["qb"]), (1, pk_, B["gk"], B["kf"], B["kb"])):
                    for dc in range(DC):
                        self.mm(pt[0:n, :], hT[:, dc, s_ * 128:s_ * 128 + n],
                                B["w_in"][:, dc, which * 512:(which + 1) * 512], dc == 0, dc == DC - 1, [hT, B["w_in"]], [pt])
                    c0 = which * 8
                    self.cp(of[0:n, :], pt[0:n, :], [pt], [of])
                    self.tt(sqk[0:n, :], of[0:n, :], of[0:n, :], ALU.mult, [of], [sqk], E=self.pool)
                    self.treduce(ssq[0:n, c0:c0 + 8], sqk[0:n, :].rearrange("p (h d) -> p h d", d=HD), [sqk], [ssq])
                    self.a(ssq[0:n, c0:c0 + 8], ssq[0:n, c0:c0 + 8], AF.Ln, [ssq, B["small"]["eps"]], [ssq],
                           scale=1.0 / HD, bias=B["small"]["eps"][0:n, 0:1])
                    self.a(ssq[0:n, c0:c0 + 8], ssq[0:n, c0:c0 + 8], AF.Exp, [ssq], [ssq], scale=-0.5)
                    self.tt(of[0:n, :].rearrange("p (h d) -> p h d", d=HD), of[0:n, :].rearrange("p (h d) -> p h d", d=HD),
                            ssq[0:n, c0:c0 + 8].unsqueeze(2).broadcast_to([n, NH, HD]), ALU.mult, [of, ssq], [of])
                    if which == 0:
                        self.tt(ob[0:n, :].rearrange("p (h d) -> p h d", d=HD), of[0:n, :].rearrange("p (h d) -> p h d", d=HD),
                                gt[0:n, :].unsqueeze(1).broadcast_to([n, NH, HD]), ALU.mult, [of, gt], [ob])
                    else:
                        self.tt(of[0:n, :].rearrange("p (h d) -> p h d", d=HD), of[0:n, :].rearrange("p (h d) -> p h d", d=HD),
                                gt[0:n, :].unsqueeze(1).broadcast_to([n, NH, HD]), ALU.mult, [of, gt], [of])
                        self.cp(ob[0:n, :], of[0:n, :], [of], [ob])
                        dst = self.pk[l, tok0 + s_ * 128:tok0 + s_ * 128 + n, :] if b is None else self.sk[l, b, :, :]
                        self.dma("sp", dst, of[0:n, :], [of], [self.pk if b is None else self.sk])
                for dc in range(DC):
                    self.mm(pv_[0:n, :], hT[:, dc, s_ * 128:s_ * 128 + n], B["w_in"][:, dc, 1024:1536], dc == 0,
                            dc == DC - 1, [hT, B["w_in"]], [pv_])
                vf = B["vf"]
                self.cp(vf[0:n, :], pv_[0:n, :], [pv_], [vf])
                dst = self.pv[l, tok0 + s_ * 128:tok0 + s_ * 128 + n, :] if b is None else self.sv[l, b, :, :]
                self.dma("sp", dst, vf[0:n, :], [vf], [self.pv if b is None else self.sv])
                self.cp(V[0:n, g0 // 128, :], vf[0:n, :], [vf], [kvb[mt]], E=self.pool)
                ptq, ptk = self.ps[1], self.ps[0]
                pbq, pbk = ptq[:].bitcast(BF16), ptk[:].bitcast(BF16)
                for c in range(4):
                    self.tr(pbq[:, c * 128:c * 128 + n], B["qb"][0:n, c * 128:(c + 1) * 128], self.idb[0:n, 0:n],
                            [B["qb"], self.idb], [ptq])
                    self.tr(pbk[:, c * 128:c * 128 + n], B["kb"][0:n, c * 128:(c + 1) * 128], self.idb[0:n, 0:n],
                            [B["kb"], self.idb], [ptk])
                self.cp(QT[:, :, s_ * 128:s_ * 128 + n], pbq[:, 0:512].rearrange("p (c t) -> p c t", c=4)[:, :, 0:n],
                        [ptq], [QT])
                self.cp(KT[:, :, g0:g0 + n], pbk[:, 0:512].rearrange("p (c t) -> p c t", c=4)[:, :, 0:n], [ptk], [kvb[mt]])
            for fc in range(4):
                pt = self.ps[fc % 2]
                for dc in range(DC):
                    self.mm(pt[:, 0:MT], B["w_in"][:, dc, 1536 + fc * 128:1536 + (fc + 1) * 128], hT[:, dc, 0:MT], dc == 0,
                            dc == DC - 1, [B["w_in"], hT], [pt])
                self.a(za[:, fc, 0:MT], pt[:, 0:MT], AF.Silu, [pt], [za])

        def kv_of(k0):
            return pastb if k0 < past else kvb[(k0 - past) // MT]

        def att(mt, extra):
            tok0 = mt * MT
            QT, za = B["QT"][mt % 2], B["za"][mt % 2]
            mix = B["mix"][mt % 2]
            blocks = []
            for s_ in reversed(range(nsub)):
                n = min(128, MT - s_ * 128)
                blocks.append((past + tok0 + s_ * 128, n, s_ * 128, True))
            for k0 in reversed(range(0, past + tok0, 128)):
                blocks.append((k0, 128, 0, False))
            units = []
            nb = len(blocks)
            for pr_ in range(NH // 2):
                for bi, blk in enumerate(blocks):
                    for hh in range(2):
                        units.append((2 * pr_ + hh, bi) + blk)
            nu = len(units)

            def slot(i):
                return self.ps[2 + (i % 5)], B["e"][i % 2], B["sp"][i % 2], B["at"][i % 2]

            def S1(i):
                h, bi, k0, nk, cs_, diag = units[i]
                c, po = h // 2, 64 * (h % 2)
                sc = slot(i)[0]
                o = sc[0:nk, cs_:MT]
                self.mm(o, KT[po:po + 64, c, k0:k0 + nk], QT[po:po + 64, c, cs_:MT], True, not diag, [kv_of(k0), QT], [sc])
                if diag:
                    self.mm(sc[0:nk, cs_:cs_ + nk], self.idb[0:nk, 0:nk], self.negm[0:nk, 0:nk], False, True,
                            [self.idb, self.negm], [sc])

            def S2a(i):
                h, bi, k0, nk, cs_, diag = units[i]
                sc, e, sp_, at = slot(i)
                if bi == 0:
                    self.ms(B["R"][h % 2][:, 0:MT], 0.0, [B["R"][h % 2]], E=self.pool)
                self.a(e[0:nk, cs_:MT], sc[0:nk, cs_:MT], AF.Exp, [sc], [e])

            def S2b(i):
                h, bi, k0, nk, cs_, diag = units[i]
                sc, e, sp_, at = slot(i)
                self.a(sp_[0:nk, cs_:MT], e[0:nk, cs_:MT], AF.Ln, [e], [sp_], bias=1.0)

            def S3(i):
                h, bi, k0, nk, cs_, diag = units[i]
                sc, e, sp_, at = slot(i)
                R = B["R"][h % 2]
                o = sc[0:nk, cs_:MT]
                self.mm(o, self.ntri[0:nk, 0:nk], sp_[0:nk, cs_:MT], False, False, [self.ntri, sp_], [sc])
                if bi > 0:
                    self.mm(o, self.nones[0:128, 0:nk], R[0:128, cs_:MT], False, True, [self.nones, R], [sc])
                if bi < nb - 1:
                    self.tt(R[0:nk, cs_:MT], R[0:nk, cs_:MT], sp_[0:nk, cs_:MT], ALU.add, [R, sp_], [R])

            def S4(i):
                h, bi, k0, nk, cs_, diag = units[i]
                sc, e, sp_, at = slot(i)
                self.a(at[0:nk, cs_:MT], sc[0:nk, cs_:MT], AF.Exp, [sc], [at])

            def S5(i):
                h, bi, k0, nk, cs_, diag = units[i]
                c, po = h // 2, 64 * (h % 2)
                sc, e, sp_, at = slot(i)
                oacc = self.ps[7]
                self.mm(oacc[po:po + 64, cs_:MT], V[0:nk, k0 // 128, h * HD:(h + 1) * HD], at[0:nk, cs_:MT], bi == 0,
                        bi == nb - 1, [kv_of(k0), at], [oacc])
                if bi == nb - 1:
                    self.tt(mix[po:po + 64, 4 + c, 0:MT], oacc[po:po + 64, 0:MT], za[po:po + 64, c, 0:MT], ALU.mult,
                            [oacc, za], [mix])

            ne = len(extra)
            done = 0
            S1(0)
            for i in range(nu + 2):
                if i % 2 == 0:
                    if i + 1 < nu:
                        S1(i + 1)
                    if i + 2 < nu:
                        S1(i + 2)
                if i < nu:
                    S2a(i)
                if 0 <= i - 2 < nu:
                    S4(i - 2)
                if i < nu:
                    S2b(i)
                if 0 <= i - 1 < nu:
                    S3(i - 1)
                if 0 <= i - 2 < nu:
                    S5(i - 2)
                tgt = (ne * (i + 1)) // max(1, nu - 2) if nu > 2 else ne
                tgt = min(ne, tgt)
                while done < tgt:
                    extra[done]()
                    done += 1
            while done < ne:
                extra[done]()
                done += 1

        def outp(mt):
            tok0 = mt * MT
            mix = B["mix"][mt % 2]
            if b is None:
                self.dma("sp", mix[:, 0:4, 0:MT], self.mixs[:, tok0:tok0 + MT].rearrange("(c p) t -> p c t", p=128),
                         [self.mixs], [mix])
            else:
                self.dma("sp", mix[:, 0:4, 0:MT], self.mixss[b].rearrange("(c p) t -> p c t", p=128), [self.mixss], [mix])
            for s_ in range(nsub):
                n = min(128, MT - s_ * 128)
                xr = B["xr"]
                self.dma("sp", xr[0:n, :], self.x_rows(xsrc, xb, tok0 + s_ * 128, n), [xsrc], [xr])
                for hf in range(2):
                    pt = self.ps[hf]
                    ot = B["ot"][hf]
                    for kc in range(DC):
                        self.mm(pt[0:n, :], mix[:, kc, s_ * 128:s_ * 128 + n], B["w_out"][:, kc, hf * 512:(hf + 1) * 512],
                                kc == 0, kc == DC - 1, [mix, B["w_out"]], [pt])
                    self.tt(ot[0:n, :], pt[0:n, :], self.gate_bc[0:n, r, hf * 512:(hf + 1) * 512],
                            ALU.mult, [pt, self.gate_bc], [ot])
                    self.tt(ot[0:n, :], ot[0:n, :], xr[0:n, hf * 512:(hf + 1) * 512], ALU.add, [ot, xr], [ot], E=self.pool)
                    self.dma("sp", self.x_rows(xdst, xb, tok0 + s_ * 128, n)[:, hf * 512:(hf + 1) * 512], ot[0:n, :],
                             [ot], [xdst])

        fe(0)
        for mt in range(nmt):
            extra = self.record(lambda: outp(mt - 1)) if mt >= 1 else []
            if mt + 1 < nmt:
                extra = extra + self.record(lambda: fe(mt + 1))
            att(mt, extra)
        outp(nmt - 1)


def host_consts():
    bf = ml_dtypes.bfloat16
    idx = np.arange(128)
    c = {}
    c["c_idb"] = np.eye(128, dtype=np.float32).astype(bf)
    c["c_idf"] = np.eye(128, dtype=np.float32)
    c["c_ntri"] = (-(idx[:, None] >= idx[None, :]).astype(np.float32)).astype(bf)
    c["c_nones"] = (-np.ones((128, 128), np.float32)).astype(bf)
    c["c_negm"] = (NEG * (idx[:, None] >= idx[None, :]).astype(np.float32)).astype(bf)
    c["c_mst"] = ((idx[:, None] // 16) <= (idx[None, :] // 16)).astype(np.float32)
    c["c_rep"] = np.tile(np.eye(16, dtype=np.float32), (1, 8))
    return c


_NC_CACHE = {}


def run(inputs, SEQ, DEPTH, NB=8, NS=2, debug=False):
    TS = inputs["x_sample"].shape[1]
    PAST = inputs["cache_k"].shape[2]
    key = (SEQ, DEPTH, NS, TS, PAST, debug)
    if key not in _NC_CACHE:
        kk = Kern(SEQ=SEQ, DEPTH=DEPTH, NS=NS, TS=TS, PAST=PAST)
        kk.debug = debug
        _NC_CACHE[key] = (kk.build(), kk.dbg_names)
    nc, dbg_names = _NC_CACHE[key]
    f = lambda a: np.ascontiguousarray(np.asarray(a, dtype=np.float32))
    consts = host_consts()
    L = DEPTH
    wmap = {
        "norm_g": f(inputs["norm_g"]), "w_mod": f(inputs["w_mod"]), "b_mod": f(inputs["b_mod"]), "w_in": f(inputs["w_in"]),
        "a_re": f(inputs["ssm_a_re"]), "a_im": f(inputs["ssm_a_im"]), "log_dt": f(inputs["ssm_log_dt"]),
        "b_re": f(inputs["ssm_b_re"]), "b_im": f(inputs["ssm_b_im"]), "c_re": f(inputs["ssm_c_re"]),
        "c_im": f(inputs["ssm_c_im"]), "ssm_d": f(inputs["ssm_d"]), "w_glu": f(inputs["w_glu"]), "b_glu": f(inputs["b_glu"]),
        "qg": f(inputs["q_norm_g"]), "kg": f(inputs["k_norm_g"]), "w_out": f(inputs["w_out"]),
    }
    xp, xs = f(inputs["x_prompt"]), f(inputs["x_sample"])
    cp_, cs_ = f(inputs["c_prompt"]), f(inputs["c_sample"])
    ck, cv = inputs["cache_k"], inputs["cache_v"]
    sre, sim = f(inputs["state_ssm_re"]), f(inputs["state_ssm_im"])
    in_maps = []
    for i in range(NB):
        sl = slice(i * NS, (i + 1) * NS)
        m = dict(wmap)
        m.update(consts)
        m["xp"] = xp[i]
        m["xs"] = xs[sl]
        m["cc"] = np.concatenate([cp_[i:i + 1], cs_[sl]], axis=0)
        m["ck"] = f(ck[:, sl]).reshape(L, NS, PAST, ATT)
        m["cv"] = f(cv[:, sl]).reshape(L, NS, PAST, ATT)
        m["sre"] = sre[:, sl]
        m["sim"] = sim[:, sl]
        in_maps.append({k: np.ascontiguousarray(v) for k, v in m.items()})
    res = run_bass_kernel_spmd(nc, in_maps, core_ids=list(range(NB)))
    R = res.results
    if debug:
        run.dbg = {n: np.asarray(R[0]["dbg_" + n]).astype(np.float32) for n in dbg_names}
    yp = np.stack([R[i]["yp"] for i in range(NB)], 0)
    ys = np.concatenate([R[i]["ys"] for i in range(NB)], 0)
    pk = np.stack([R[i]["pk"] for i in range(NB)], 1).reshape(L, NB, SEQ, NH, HD)
    pv = np.stack([R[i]["pv"] for i in range(NB)], 1).reshape(L, NB, SEQ, NH, HD)
    pr = np.stack([R[i]["pr"] for i in range(NB)], 1)
    pi = np.stack([R[i]["pi"] for i in range(NB)], 1)
    sk = np.concatenate([R[i]["sk"] for i in range(NB)], 1).reshape(L, NB * NS, TS, NH, HD)
    sv = np.concatenate([R[i]["sv"] for i in range(NB)], 1).reshape(L, NB * NS, TS, NH, HD)
    sr = np.concatenate([R[i]["sr"] for i in range(NB)], 1)
    si = np.concatenate([R[i]["si"] for i in range(NB)], 1)
    return tuple(np.asarray(a, dtype=np.float32) for a in (yp, ys, pk, pv, pr, pi, sk, sv, sr, si))


def kernel(**inputs):
    SEQ = inputs["x_prompt"].shape[1]
    DEPTH = inputs["w_in"].shape[0]
    return run(inputs, SEQ, DEPTH)
```

```python
import contextlib
import math

import ml_dtypes
import numpy as np

import concourse.bass as bass
import concourse.mybir as mybir
from concourse.alu_op_type import AluOpType as ALU
from concourse.bass_utils import run_bass_kernel_spmd

F32 = mybir.dt.float32
BF16 = mybir.dt.bfloat16
AF = mybir.ActivationFunctionType
AX = mybir.AxisListType

D = 1024
DC = 8
NH = 8
HD = 64
G = 32
PST = 64
ATT = 512
SSMW = 512
EPS = 1e-6
NEG = -30000.0
MAGIC = 12582912.0
TWO_PI = 2.0 * math.pi
C1 = 6.28125
C2 = TWO_PI - 6.28125


class Buf:
    __slots__ = ("w", "r")

    def __init__(self):
        self.w = {}
        self.r = {}


class Eng:
    def __init__(self, kern, name, eng, safe):
        self.k = kern
        self.name = name
        self.eng = eng
        self.safe = safe
        self.sem = kern.new_sem(name)
        self.cnt = 0
        self.waited = {}
        self.nsem = 0


class T:
    def __init__(self, h, nbuf=1):
        self.h = h
        self.b = Buf()

    def __getitem__(self, idx):
        return self.h[idx]


class Kern:
    def __init__(self, SEQ=4096, DEPTH=4, NS=2, TS=16, PAST=1024):
        self.SEQ, self.DEPTH, self.NS, self.TS, self.PAST = SEQ, DEPTH, NS, TS, PAST
        self.nc = bass.Bass("TRN2", target_bir_lowering=False)
        self.es = contextlib.ExitStack()
        self.nsem = 0
        nc = self.nc
        self.pe = Eng(self, "pe", nc.tensor, True)
        self.act = Eng(self, "act", nc.scalar, False)
        self.dve = Eng(self, "dve", nc.vector, False)
        self.pool = Eng(self, "pool", nc.gpsimd, False)
        self.sp = Eng(self, "sp", nc.sync, True)
        self.engs = [self.pe, self.act, self.dve, self.pool, self.sp]
        self.dsems = {"sp": [], "pool": []}
        for q, n in (("sp", 24), ("pool", 10)):
            for i in range(n):
                self.dsems[q].append([self.new_sem(f"d{q}{i}"), 0])
        self.dnext = {"sp": 0, "pool": 0}
        self.uid = 0
        self.debug = False
        self.dbg_names = []
        self.rec = None

    def new_sem(self, name):
        self.nsem += 1
        return self.es.enter_context(self.nc.semaphore(f"{name}_{self.nsem}"))

    def _wait(self, E, sem, val):
        if E.waited.get(id(sem), 0) >= val:
            return
        E.eng.wait_ge(sem, val)
        E.waited[id(sem)] = val

    def _deps(self, E, reads, writes):
        for b in reads:
            for sem, val in b.w.values():
                if sem is E.sem and E.safe:
                    continue
                self._wait(E, sem, val)
        for b in writes:
            toks = list(b.r.values()) + list(b.w.values())
            for sem, val in toks:
                if sem is E.sem and E.safe:
                    continue
                self._wait(E, sem, val)

    def _mark(self, tok, reads, writes, is_dma=False):
        for b in reads:
            b.r[id(tok[0])] = tok
        for b in writes:
            if is_dma:
                b.w[id(tok[0])] = tok
            else:
                b.w = {id(tok[0]): tok}
            b.r = {}

    def op(self, E, fn, reads=(), writes=()):
        if self.rec is not None:
            self.rec.append(lambda: self.op(E, fn, reads, writes))
            return
        reads = [x.b if isinstance(x, T) else x for x in reads]
        writes = [x.b if isinstance(x, T) else x for x in writes]
        self._deps(E, reads, writes)
        inst = fn()
        E.cnt += 1
        inst.then_inc(E.sem, 1)
        self._mark((E.sem, E.cnt), reads, writes)
        if E.cnt >= 30000:
            E.sem = self.new_sem(E.name)
            E.cnt = 0

    def dma(self, q, out, in_, reads=(), writes=(), **kw):
        if self.rec is not None:
            self.rec.append(lambda: self.dma(q, out, in_, reads, writes, **kw))
            return
        E = self.sp if q == "sp" else self.pool
        reads = [x.b if isinstance(x, T) else x for x in reads]
        writes = [x.b if isinstance(x, T) else x for x in writes]
        self._deps(E, reads, writes)
        lst = self.dsems[q]
        i = self.dnext[q]
        self.dnext[q] = (i + 1) % len(lst)
        ent = lst[i]
        self._wait(E, ent[0], ent[1])
        inst = E.eng.dma_start(out=out, in_=in_, **kw)
        ent[1] += 16
        inst.then_inc(ent[0], 16)
        self._mark((ent[0], ent[1]), reads, writes, is_dma=True)

    def dbg(self, name, t, ap, shape, dt=F32):
        if not getattr(self, "debug", False):
            return
        if name in self.dbg_names:
            return
        self.dbg_names.append(name)
        d = self.dram("dbg_" + name, list(shape), dt, "ExternalOutput")
        idx = tuple(slice(None) for _ in shape)
        self.dma("sp", d[idx], ap, [t], [d])

    def record(self, fn):
        assert self.rec is None
        self.rec = []
        try:
            fn()
        finally:
            out, self.rec = self.rec, None
        return out

    def barrier(self):
        toks = [(e.sem, e.cnt) for e in self.engs if e.cnt > 0]
        for q in ("sp", "pool"):
            for ent in self.dsems[q]:
                if ent[1] > 0:
                    toks.append((ent[0], ent[1]))
        for e in self.engs:
            for sem, val in toks:
                if sem is e.sem and e.safe:
                    continue
                self._wait(e, sem, val)

    def sb(self, shape, dt, name=None):
        self.uid += 1
        return T(self.es.enter_context(self.nc.sbuf_tensor(f"{name or 't'}_{self.uid}", list(shape), dt)))

    def sb_in(self, stack, shape, dt, name=None):
        self.uid += 1
        return T(stack.enter_context(self.nc.sbuf_tensor(f"{name or 't'}_{self.uid}", list(shape), dt)))

    def dram(self, name, shape, dt, kind):
        return T(self.nc.dram_tensor(name, list(shape), dt, kind=kind).ap())

    def mm(self, out, lhsT, rhs, start, stop, reads, writes):
        self.op(self.pe, lambda: self.nc.tensor.matmul(out, lhsT, rhs, start=start, stop=stop,
                                                       skip_group_check=True), reads, writes)

    def tr(self, out, in_, ident, reads, writes):
        self.op(self.pe, lambda: self.nc.tensor.transpose(out, in_, ident), reads, writes)

    def a(self, out, in_, func, reads, writes, **kw):
        self.op(self.act, lambda: self.nc.scalar.activation(out=out, in_=in_, func=func, **kw), reads, writes)

    def tt(self, out, in0, in1, op, reads, writes, E=None):
        E = E or self.dve
        self.op(E, lambda: E.eng.tensor_tensor(out=out, in0=in0, in1=in1, op=op), reads, writes)

    def ts(self, out, in0, s1, s2, op0, op1, reads, writes, E=None):
        E = E or self.dve
        if op1 is None:
            self.op(E, lambda: E.eng.tensor_scalar(out=out, in0=in0, scalar1=s1, scalar2=None, op0=op0), reads, writes)
        else:
            self.op(E, lambda: E.eng.tensor_scalar(out=out, in0=in0, scalar1=s1, scalar2=s2, op0=op0, op1=op1),
                    reads, writes)

    def stt(self, out, in0, scalar, in1, op0, op1, reads, writes):
        self.op(self.dve, lambda: self.nc.vector.scalar_tensor_tensor(out=out, in0=in0, scalar=scalar, in1=in1,
                                                                     op0=op0, op1=op1), reads, writes)

    def cp(self, out, in_, reads, writes, E=None):
        E = E or self.dve
        if E is self.act:
            self.op(E, lambda: self.nc.scalar.copy(out=out, in_=in_), reads, writes)
        else:
            self.op(E, lambda: E.eng.tensor_copy(out=out, in_=in_), reads, writes)

    def ms(self, ap, val, writes, E=None):
        E = E or self.dve
        self.op(E, lambda: E.eng.memset(ap, val), (), writes)

    def declare_io(self):
        SEQ, L, NS, TS, PAST = self.SEQ, self.DEPTH, self.NS, self.TS, self.PAST
        d = self.dram
        I = "ExternalInput"
        O = "ExternalOutput"
        self.xp = d("xp", [SEQ, D], F32, I)
        self.xs = d("xs", [NS, TS, D], F32, I)
        self.cc = d("cc", [1 + NS, D], F32, I)
        self.ck = d("ck", [L, NS, PAST, ATT], F32, I)
        self.cv = d("cv", [L, NS, PAST, ATT], F32, I)
        self.sre = d("sre", [L, NS, G, PST], F32, I)
        self.sim = d("sim", [L, NS, G, PST], F32, I)
        self.norm_g = d("norm_g", [L, D], F32, I)
        self.w_mod = d("w_mod", [L, D, 3 * D], F32, I)
        self.b_mod = d("b_mod", [L, 3 * D], F32, I)
        self.w_in = d("w_in", [L, D, 3 * D], F32, I)
        self.a_re = d("a_re", [L, G, PST], F32, I)
        self.a_im = d("a_im", [L, G, PST], F32, I)
        self.log_dt = d("log_dt", [L, G], F32, I)
        self.b_re = d("b_re", [L, G, PST, 16], F32, I)
        self.b_im = d("b_im", [L, G, PST, 16], F32, I)
        self.c_re = d("c_re", [L, G, 16, PST], F32, I)
        self.c_im = d("c_im", [L, G, 16, PST], F32, I)
        self.ssm_d = d("ssm_d", [L, SSMW], F32, I)
        self.w_glu = d("w_glu", [L, SSMW, SSMW], F32, I)
        self.b_glu = d("b_glu", [L, SSMW], F32, I)
        self.qg = d("qg", [L, HD], F32, I)
        self.kg = d("kg", [L, HD], F32, I)
        self.w_out = d("w_out", [L, D, D], F32, I)
        self.c_idb = d("c_idb", [128, 128], BF16, I)
        self.c_idf = d("c_idf", [128, 128], F32, I)
        self.c_ntri = d("c_ntri", [128, 128], BF16, I)
        self.c_nones = d("c_nones", [128, 128], BF16, I)
        self.c_negm = d("c_negm", [128, 128], BF16, I)
        self.c_mst = d("c_mst", [128, 128], F32, I)
        self.c_rep = d("c_rep", [16, 128], F32, I)
        self.yp = d("yp", [SEQ, D], F32, O)
        self.ys = d("ys", [NS, TS, D], F32, O)
        self.pk = d("pk", [L, SEQ, ATT], F32, O)
        self.pv = d("pv", [L, SEQ, ATT], F32, O)
        self.pr = d("pr", [L, G, PST], F32, O)
        self.pi = d("pi", [L, G, PST], F32, O)
        self.sk = d("sk", [L, NS, TS, ATT], F32, O)
        self.sv = d("sv", [L, NS, TS, ATT], F32, O)
        self.sr = d("sr", [L, NS, G, PST], F32, O)
        self.si = d("si", [L, NS, G, PST], F32, O)
        self.mixs = d("mixs", [SSMW, SEQ], BF16, "Internal")
        self.mixss = d("mixss", [NS, SSMW, TS], BF16, "Internal")

    def build(self):
        nc = self.nc
        self.declare_io()
        sb = self.sb
        self.ps = []
        for i in range(8):
            self.ps.append(T(self.es.enter_context(nc.psum_tensor(f"psb{i}", [128, 512], F32))))
        self.idb = sb([128, 128], BF16, "idb")
        self.idf = sb([128, 128], F32, "idf")
        self.ntri = sb([128, 128], BF16, "ntri")
        self.nones = sb([128, 128], BF16, "nones")
        self.negm = sb([128, 128], BF16, "negm")
        self.mst = sb([128, 128], F32, "mst")
        self.rep = sb([16, 128], F32, "rep")
        for t, src in ((self.idb, self.c_idb), (self.idf, self.c_idf), (self.ntri, self.c_ntri),
                       (self.nones, self.c_nones), (self.negm, self.c_negm), (self.mst, self.c_mst)):
            self.dma("sp", t[:], src[:, :], [src], [t])
        self.dma("sp", self.rep[:], self.c_rep[:, :], [self.c_rep], [self.rep])
        self.onesf = sb([1, 128], F32, "onesf")
        self.ms(self.onesf[:], 1.0, [self.onesf])
        NR = 1 + self.NS
        self.NR = NR
        self.scT = sb([128, DC, 4], BF16, "scT")
        self.ms(self.scT[:], 0.0, [self.scT])
        with contextlib.ExitStack() as st0:
            crow = self.sb_in(st0, [NR, D], F32, "crow")
            self.dma("sp", crow[:], self.cc[:, :], [self.cc], [crow])
            srow = self.sb_in(st0, [NR, D], F32, "srow")
            self.a(srow[:], crow[:], AF.Silu, [crow], [srow])
            pt = self.ps[0]
            for dc in range(DC):
                self.tr(pt[:, dc * 4:dc * 4 + NR], srow[:, dc * 128:(dc + 1) * 128], self.idf[0:NR, 0:NR], [srow, self.idf], [pt])
            for dc in range(DC):
                self.cp(self.scT[:, dc, 0:NR], pt[:, dc * 4:dc * 4 + NR], [pt], [self.scT])
            self.barrier()
        self.gate_bc = sb([128, NR, D], F32, "gate_bc")
        self.ascale = sb([128, NR, DC], F32, "ascale")
        self.shiftT = sb([128, NR, DC], F32, "shiftT")
        self.normgT = sb([128, DC, 2], F32, "normgT")

        seqs = [dict(name="p", T=self.SEQ, MT=min(512, self.SEQ), r=0, past=0, b=None)]
        for b in range(self.NS):
            seqs.append(dict(name=f"s{b}", T=self.TS, MT=self.TS, r=1 + b, past=self.PAST, b=b))
        self.seqs = seqs

        for l in range(self.DEPTH):
            self.mod_phase(l)
            self.barrier()
            with contextlib.ExitStack() as st:
                self.passA_setup(st, l)
                for sq in seqs:
                    self.passA(st, sq, l)
                self.barrier()
            with contextlib.ExitStack() as st:
                self.passB_setup(st, l)
                for sq in seqs:
                    self.passB(st, sq, l)
                self.barrier()
        self.barrier()
        self.es.close()
        return nc

    def mod_phase(self, l):
        NR = self.NR
        with contextlib.ExitStack() as st:
            wm = self.sb_in(st, [128, DC, D], BF16, "wm")
            self.bmodrow = self.sb_in(st, [1, 3 * D], F32, "bmodrow")
            self.ngrow = self.sb_in(st, [1, D], F32, "ngrow")
            self.screp = self.sb_in(st, [128, NR, DC, 128], BF16, "screp")
            for r in range(NR):
                for dc in range(DC):
                    self.cp(self.screp[:, r, dc, :], self.scT[:, dc, r:r + 1].broadcast_to([128, 128]), [self.scT],
                            [self.screp])
            self.dma("sp", self.bmodrow[:], self.b_mod[l:l + 1, :], [self.b_mod], [self.bmodrow])
            self.dma("sp", self.ngrow[:], self.norm_g[l:l + 1, :], [self.norm_g], [self.ngrow])
            modT = self.sb_in(st, [128, 2, DC, 4], F32, "modT")
            for blk in range(3):
                for h2 in range(2):
                    self.dma("pool", wm[:, h2 * 4:(h2 + 1) * 4, :],
                             self.w_mod[l, h2 * 512:(h2 + 1) * 512, blk * D:(blk + 1) * D].rearrange("(c p) f -> p c f", p=128),
                             [self.w_mod], [wm])
                if blk < 2:
                    pt = self.ps[1]
                    for fc in range(DC):
                        o = pt[:, fc * 4:fc * 4 + 4]
                        for dc in range(DC):
                            self.mm(o, wm[:, dc, fc * 128:(fc + 1) * 128], self.scT[:, dc, 0:4], dc == 0, False,
                                    [wm, self.scT], [pt])
                        self.mm(o, self.bmodrow[0:1, blk * D + fc * 128: blk * D + (fc + 1) * 128],
                                self.onesf[0:1, 0:4], False, True, [self.bmodrow, self.onesf], [pt])
                    self.cp(modT[:, blk, :, :], pt[:, 0:DC * 4].rearrange("p (f r) -> p f r", r=4), [pt], [modT])
                else:
                    for r in range(NR):
                        for hf in range(2):
                            pt = self.ps[2 + hf]
                            for dc in range(DC):
                                self.mm(pt[:, :], self.screp[:, r, dc, :], wm[:, dc, hf * 512:(hf + 1) * 512], dc == 0, False,
                                        [self.screp, wm], [pt])
                            self.mm(pt[:, :], self.onesf[0:1, 0:128],
                                    self.bmodrow[0:1, 2 * D + hf * 512: 2 * D + (hf + 1) * 512], False, True,
                                    [self.onesf, self.bmodrow], [pt])
                            self.cp(self.gate_bc[:, r, hf * 512:(hf + 1) * 512], pt[:, :], [pt], [self.gate_bc])
            pt = self.ps[4]
            for fc in range(DC):
                self.mm(pt[:, fc * 2:fc * 2 + 2], self.ngrow[0:1, fc * 128:(fc + 1) * 128], self.onesf[0:1, 0:2], True, True,
                        [self.ngrow, self.onesf], [pt])
            self.cp(self.normgT[:], pt[:, 0:DC * 2].rearrange("p (f r) -> p f r", r=2), [pt], [self.normgT])
            for r in range(NR):
                self.stt(self.ascale[:, r, :], modT[:, 1, :, r], 1.0, self.normgT[:, :, 0], ALU.add, ALU.mult,
                         [modT, self.normgT], [self.ascale])
                self.cp(self.shiftT[:, r, :], modT[:, 0, :, r], [modT], [self.shiftT])
            self.barrier()

    def norm_tile(self, src_ap, src_t, n, r, xt, xn, hT, col0, small, pbank, light=False, all_act=False):
        self.dma("sp", xt[0:n, :], src_ap, [src_t], [xt])
        ss = small["ss"]
        self.a(xn[0:n, :], xt[0:n, :], AF.Square, [xt], [xn, ss], accum_out=ss[0:n, 0:1])
        self.a(ss[0:n, 1:2], ss[0:n, 0:1], AF.Ln, [ss, small["eps"]], [ss], scale=1.0 / D, bias=small["eps"][0:n, 0:1])
        self.a(ss[0:n, 2:3], ss[0:n, 1:2], AF.Exp, [ss], [ss], scale=-0.5)
        if light:
            self.ts(xn[0:n, :], xt[0:n, :], ss[0:n, 2:3], None, ALU.mult, None, [xt, ss], [xn])
        else:
            self.a(xn[0:n, :], xt[0:n, :], AF.Copy, [xt, ss], [xn], scale=ss[0:n, 2:3])
        pb = pbank[:].bitcast(BF16)
        for dc in range(DC):
            self.tr(pb[:, dc * 128:dc * 128 + n], xn[0:n, dc * 128:(dc + 1) * 128], self.idb[0:n, 0:n],
                    [xn, self.idb], [pbank])
        for dc in range(DC):
            E = self.act if all_act else (self.dve if (dc % 2 == 0 or light) else self.act)
            if E is self.dve:
                self.ts(hT[:, dc, col0:col0 + n], pb[:, dc * 128:dc * 128 + n], self.ascale[:, r, dc:dc + 1],
                        self.shiftT[:, r, dc:dc + 1], ALU.mult, ALU.add, [pbank, self.ascale, self.shiftT], [hT])
            else:
                self.a(hT[:, dc, col0:col0 + n], pb[:, dc * 128:dc * 128 + n], AF.Identity,
                       [pbank, self.ascale, self.shiftT], [hT], scale=self.ascale[:, r, dc:dc + 1],
                       bias=self.shiftT[:, r, dc:dc + 1])

    def mk_small(self, st):
        sm = dict(ss=self.sb_in(st, [128, 4], F32, "ss"),
                  eps=self.sb_in(st, [128, 1], F32, "eps"))
        self.ms(sm["eps"][:], EPS, [sm["eps"]])
        return sm

    def x_src(self, sq, l):
        if sq["b"] is None:
            return (self.xp if l == 0 else self.yp), None
        return (self.xs if l == 0 else self.ys), sq["b"]

    def x_rows(self, t, b, r0, n):
        if b is None:
            return t[r0:r0 + n, :]
        return t[b, r0:r0 + n, :]

    def passA_setup(self, st, l):
        sbi = lambda shape, dt, name: self.sb_in(st, shape, dt, name)
        A = {}
        self.A = A
        A["w_in"] = sbi([128, DC, 1024], BF16, "w_inA")
        for h2 in range(2):
            self.dma("pool", A["w_in"][:, h2 * 4:(h2 + 1) * 4, :],
                     self.w_in[l, h2 * 512:(h2 + 1) * 512, 0:1024].rearrange("(c p) f -> p c f", p=128),
                     [self.w_in], [A["w_in"]])
        A["w_glu"] = sbi([128, 4, SSMW], BF16, "w_glu")
        self.dma("pool", A["w_glu"][:], self.w_glu[l].rearrange("(c p) f -> p c f", p=128), [self.w_glu], [A["w_glu"]])
        A["bglu"] = sbi([128, 4], F32, "bglu")
        self.dma("sp", A["bglu"][:], self.b_glu[l].rearrange("(c p) -> p c", p=128), [self.b_glu], [A["bglu"]],
                 allow_slow_non_contiguous=True)
        A["WG"] = sbi([128, 2, G, 128], BF16, "WG")
        A["W0"] = sbi([128, G, 8, 32], BF16, "W0")
        A["WY"] = sbi([128, G, 8, 32], BF16, "WY")
        A["W0b"] = sbi([128, 8, 8, 64], BF16, "W0b")
        A["WYb"] = sbi([128, 8, 8, 64], BF16, "WYb")
        self.ms(A["W0b"][:], 0.0, [A["W0b"]])
        self.ms(A["WYb"][:], 0.0, [A["WYb"]], E=self.pool)
        A["CR"] = sbi([128, 64], F32, "CR")
        A["CI"] = sbi([128, 64], F32, "CI")
        A["CR2"] = sbi([128, 64], F32, "CR2")
        A["CI2"] = sbi([128, 64], F32, "CI2")
        A["q1"] = sbi([128, 8, 64], F32, "q1")
        A["q2"] = sbi([128, 8, 64], F32, "q2")
        A["q3"] = sbi([128, 8, 64], F32, "q3")
        A["q4"] = sbi([128, 8, 64], F32, "q4")
        self.ms(A["W0"][:], 0.0, [A["W0"]])
        self.ms(A["WY"][:], 0.0, [A["WY"]], E=self.pool)
        self.s5_precompute(l, A)
        self.barrier()
        A["xt"] = sbi([128, D], F32, "xtA")
        A["xn"] = sbi([128, D], BF16, "xnA")
        A["hT"] = sbi([128, DC, 512], BF16, "hTA")
        A["XX"] = sbi([64, G, 8, 16], BF16, "XX")
        A["U8"] = [sbi([128, G, 64], BF16, "U8a"), sbi([128, G, 64], BF16, "U8b")]
        A["zs"] = [sbi([128, 4, 512], BF16, "zsa"), sbi([128, 4, 512], BF16, "zsb")]
        A["big"] = [sbi([128, 4096], F32, "bigA"), sbi([128, 4096], F32, "bigB")]
        A["H"] = sbi([128, 65, 64], F32, "H")
        A["t1"] = sbi([128, 64], F32, "t1")
        A["t2"] = sbi([128, 64], F32, "t2")
        A["S0"] = sbi([128, G, 64], BF16, "S0")
        A["gs"] = sbi([128, 4, 512], BF16, "gsA")
        A["small"] = self.mk_small(st)
        A["st32"] = sbi([32, 256], F32, "st32")
        A["fin"] = sbi([32, 128], F32, "fin")

    def s5_precompute(self, l, A):
        with contextlib.ExitStack() as st:
            sbi = lambda shape, dt, name: self.sb_in(st, shape, dt, name)
            V = {}

            def v(name):
                if name not in V:
                    V[name] = sbi([128, G], F32, name)
                return V[name]

            dve_tt = self.tt
            araw = sbi([32, 256], F32, "araw")
            for j in range(2):
                self.dma("sp", araw[:, j * 64:(j + 1) * 64], self.a_re[l], [self.a_re], [araw])
                self.dma("sp", araw[:, 128 + j * 64:128 + (j + 1) * 64], self.a_im[l], [self.a_im], [araw])
            pt = self.ps[0]
            self.tr(pt[:, 0:32], araw[:, 0:128], self.idf[0:32, 0:32], [araw, self.idf], [pt])
            self.tr(pt[:, 32:64], araw[:, 128:256], self.idf[0:32, 0:32], [araw, self.idf], [pt])
            lr, li = v("lr"), v("li")
            self.cp(lr[:], pt[:, 0:32], [pt], [lr])
            self.cp(li[:], pt[:, 32:64], [pt], [li])
            dtb = v("dtb")
            self.dma("sp", dtb[:], self.log_dt[l:l + 1, :].broadcast_to([128, G]), [self.log_dt], [dtb])
            self.a(dtb[:], dtb[:], AF.Exp, [dtb], [dtb])
            rd, mag, ang = v("rd"), v("mag"), v("ang")
            dve_tt(rd[:], lr[:], dtb[:], ALU.mult, [lr, dtb], [rd])
            self.a(mag[:], rd[:], AF.Exp, [rd], [mag])
            dve_tt(ang[:], li[:], dtb[:], ALU.mult, [li, dtb], [ang])
            kq, r_, q_, q2 = v("kq"), v("r_"), v("q_"), v("q2")
            self.ts(kq[:], ang[:], 1.0 / TWO_PI, MAGIC, ALU.mult, ALU.add, [ang], [kq])
            self.ts(kq[:], kq[:], MAGIC, None, ALU.subtract, None, [kq], [kq])
            self.stt(r_[:], kq[:], -C1, ang[:], ALU.mult, ALU.add, [kq, ang], [r_])
            self.stt(r_[:], kq[:], -C2, r_[:], ALU.mult, ALU.add, [kq, r_], [r_])
            self.ts(q_[:], r_[:], 0.25, None, ALU.mult, None, [r_], [q_])
            dve_tt(q2[:], q_[:], q_[:], ALU.mult, [q_], [q2])
            sn, cs, p_ = v("sn"), v("cs"), v("p_")
            self.ts(p_[:], q2[:], 1.0 / 362880.0, None, ALU.mult, None, [q2], [p_])
            for c in (-1.0 / 5040.0, 1.0 / 120.0, -1.0 / 6.0):
                self.stt(p_[:], p_[:], c, q2[:], ALU.add, ALU.mult, [p_, q2], [p_])
            self.stt(sn[:], p_[:], 1.0, q_[:], ALU.add, ALU.mult, [p_, q_], [sn])
            self.ts(p_[:], q2[:], -1.0 / 3628800.0, None, ALU.mult, None, [q2], [p_])
            for c in (1.0 / 40320.0, -1.0 / 720.0, 1.0 / 24.0, -0.5):
                self.stt(p_[:], p_[:], c, q2[:], ALU.add, ALU.mult, [p_, q2], [p_])
            self.ts(cs[:], p_[:], 1.0, None, ALU.add, None, [p_], [cs])
            t_a, t_b = v("t_a"), v("t_b")
            for _ in range(2):
                dve_tt(t_a[:], cs[:], cs[:], ALU.mult, [cs], [t_a])
                dve_tt(t_b[:], sn[:], sn[:], ALU.mult, [sn], [t_b])
                self.stt(sn[:], cs[:], 2.0, sn[:], ALU.mult, ALU.mult, [cs, sn], [sn])
                dve_tt(cs[:], t_a[:], t_b[:], ALU.subtract, [t_a, t_b], [cs])
            PW = sbi([128, 2, 9, G], F32, "PW")
            QW = sbi([128, 2, 8, G], F32, "QW")
            abr, abi = PW[:, 0, 1, :], PW[:, 1, 1, :]
            self.ms(PW[:, 0, 0, :], 1.0, [PW])
            self.ms(PW[:, 1, 0, :], 0.0, [PW])
            dve_tt(abr, mag[:], cs[:], ALU.mult, [mag, cs], [PW])
            dve_tt(abi, mag[:], sn[:], ALU.mult, [mag, sn], [PW])

            def cmul(dst, dr, di, src, ar, ai, br, bi, extra):
                dve_tt(t_a[:], ar, br, ALU.mult, [src] + extra, [t_a])
                dve_tt(t_b[:], ai, bi, ALU.mult, [src] + extra, [t_b])
                dve_tt(dr, t_a[:], t_b[:], ALU.subtract, [t_a, t_b], [dst])
                dve_tt(t_a[:], ar, bi, ALU.mult, [src] + extra, [t_a])
                dve_tt(t_b[:], ai, br, ALU.mult, [src] + extra, [t_b])
                dve_tt(di, t_a[:], t_b[:], ALU.add, [t_a, t_b], [dst])

            for k in range(1, 8):
                cmul(PW, PW[:, 0, k + 1, :], PW[:, 1, k + 1, :], PW, PW[:, 0, k, :], PW[:, 1, k, :], abr, abi, [])
            m2, ivr, ivi = v("m2"), v("ivr"), v("ivi")
            dve_tt(m2[:], mag[:], mag[:], ALU.mult, [mag], [m2])
            self.op(self.dve, lambda: self.nc.vector.reciprocal(out=m2[:], in_=m2[:]), [m2.b], [m2.b])
            dve_tt(ivr[:], abr, m2[:], ALU.mult, [PW, m2], [ivr])
            self.stt(ivi[:], abi, -1.0, m2[:], ALU.mult, ALU.mult, [PW, m2], [ivi])
            self.ms(QW[:, 0, 0, :], 1.0, [QW])
            self.ms(QW[:, 1, 0, :], 0.0, [QW])
            self.cp(QW[:, 0, 1, :], ivr[:], [ivr], [QW])
            self.cp(QW[:, 1, 1, :], ivi[:], [ivi], [QW])
            for k in range(1, 7):
                cmul(QW, QW[:, 0, k + 1, :], QW[:, 1, k + 1, :], QW, QW[:, 0, k, :], QW[:, 1, k, :], ivr[:], ivi[:],
                     [ivr, ivi])
            self.dbg("PW", PW, PW[:], [128, 2, 9, G])
            self.dbg("QW", QW, QW[:], [128, 2, 8, G])
            self.dbg("cs", cs, cs[:], [128, G])
            self.dbg("sn", sn, sn[:], [128, G])
            self.dbg("mag", mag, mag[:], [128, G])
            self.dbg("ang", ang, ang[:], [128, G])
            self.cp(A["CR"][:, 0:32], PW[:, 0, 8, :], [PW], [A["CR"]])
            self.cp(A["CR"][:, 32:64], PW[:, 0, 8, :], [PW], [A["CR"]])
            self.cp(A["CI"][:, 0:32], PW[:, 1, 8, :], [PW], [A["CI"]])
            self.ts(A["CI"][:, 32:64], PW[:, 1, 8, :], -1.0, None, ALU.mult, None, [PW], [A["CI"]])
            p16r, p16i = v("p16r"), v("p16i")
            cmul(p16r, p16r[:], p16i[:], PW, PW[:, 0, 8, :], PW[:, 1, 8, :], PW[:, 0, 8, :], PW[:, 1, 8, :], [])
            self.cp(A["CR2"][:, 0:32], p16r[:], [p16r], [A["CR2"]])
            self.cp(A["CR2"][:, 32:64], p16r[:], [p16r], [A["CR2"]])
            self.cp(A["CI2"][:, 0:32], p16i[:], [p16i], [A["CI2"]])
            self.ts(A["CI2"][:, 32:64], p16i[:], -1.0, None, ALU.mult, None, [p16i], [A["CI2"]])
            den, nr, fr, fi = v("den"), v("nr"), v("fr"), v("fi")
            dve_tt(t_a[:], lr[:], lr[:], ALU.mult, [lr], [t_a])
            dve_tt(t_b[:], li[:], li[:], ALU.mult, [li], [t_b])
            dve_tt(den[:], t_a[:], t_b[:], ALU.add, [t_a, t_b], [den])
            self.op(self.dve, lambda: self.nc.vector.reciprocal(out=den[:], in_=den[:]), [den.b], [den.b])
            self.ts(nr[:], abr, -1.0, None, ALU.add, None, [PW], [nr])
            dve_tt(t_a[:], nr[:], lr[:], ALU.mult, [nr, lr], [t_a])
            dve_tt(t_b[:], abi, li[:], ALU.mult, [PW, li], [t_b])
            dve_tt(t_a[:], t_a[:], t_b[:], ALU.add, [t_a, t_b], [t_a])
            dve_tt(fr[:], t_a[:], den[:], ALU.mult, [t_a, den], [fr])
            dve_tt(t_a[:], abi, lr[:], ALU.mult, [PW, lr], [t_a])
            dve_tt(t_b[:], nr[:], li[:], ALU.mult, [nr, li], [t_b])
            dve_tt(t_a[:], t_a[:], t_b[:], ALU.subtract, [t_a, t_b], [t_a])
            dve_tt(fi[:], t_a[:], den[:], ALU.mult, [t_a, den], [fi])
            Bs = sbi([128, G, 16], F32, "Bs")
            Br = sbi([128, G, 16], F32, "Br")
            bre = self.b_re[l].rearrange("g p c -> p g c")
            bim = self.b_im[l].rearrange("g p c -> p g c")
            self.dma("sp", Bs[0:64, :, :], bre, [self.b_re], [Bs])
            self.dma("sp", Bs[64:128, :, :], bim, [self.b_im], [Bs])
            self.dma("sp", Br[0:64, :, :], bim, [self.b_im], [Br])
            self.dma("sp", Br[64:128, :, :], bre, [self.b_re], [Br])
            self.ts(Br[0:64, :, :], Br[0:64, :, :], -1.0, None, ALU.mult, None, [Br], [Br])
            Bbs = sbi([128, G, 16], F32, "Bbs")
            Bbr = sbi([128, G, 16], F32, "Bbr")
            tm1 = sbi([128, G, 16], F32, "tm1")
            tm2 = sbi([128, G, 16], F32, "tm2")

            def bc(ap2):
                return ap2.unsqueeze(2).broadcast_to([128, G, 16])

            dve_tt(tm1[:], Bs[:], bc(fr[:]), ALU.mult, [Bs, fr], [tm1])
            dve_tt(tm2[:], Br[:], bc(fi[:]), ALU.mult, [Br, fi], [tm2])
            dve_tt(Bbs[:], tm1[:], tm2[:], ALU.add, [tm1, tm2], [Bbs])
            dve_tt(tm1[:], Br[:], bc(fr[:]), ALU.mult, [Br, fr], [tm1])
            dve_tt(tm2[:], Bs[:], bc(fi[:]), ALU.mult, [Bs, fi], [tm2])
            dve_tt(Bbr[:], tm1[:], tm2[:], ALU.subtract, [tm1, tm2], [Bbr])
            scrA = sbi([128, G, 8, 16], F32, "scrA")
            scrB = sbi([128, G, 8, 16], F32, "scrB")
            for s in range(8):
                pr_, pi_ = PW[:, 0, 7 - s, :], PW[:, 1, 7 - s, :]
                dve_tt(tm1[:], Bbs[:], bc(pr_), ALU.mult, [Bbs, PW], [tm1])
                dve_tt(tm2[:], Bbr[:], bc(pi_), ALU.mult, [Bbr, PW], [tm2])
                dve_tt(scrA[:, :, s, :], tm1[:], tm2[:], ALU.add, [tm1, tm2], [scrA])
                dve_tt(tm1[:], Bbr[:], bc(pr_), ALU.mult, [Bbr, PW], [tm1])
                dve_tt(tm2[:], Bbs[:], bc(pi_), ALU.mult, [Bbs, PW], [tm2])
                dve_tt(scrB[:, :, s, :], tm1[:], tm2[:], ALU.subtract, [tm1, tm2], [scrB])
            for vi, scr in enumerate((scrA, scrB)):
                for g0 in range(0, G, 4):
                    pt = self.ps[(g0 // 4) % 4]
                    for j in range(4):
                        self.tr(pt[:, j * 128:(j + 1) * 128], scr[:, g0 + j, :, :].rearrange("p s c -> p (s c)"), self.idf[:, :], [scr, self.idf], [pt])
                    E = self.act if (g0 // 4) % 2 else self.dve
                    self.cp(A["WG"][:, vi, g0:g0 + 4, :], pt[:, :].rearrange("p (g f) -> p g f", g=4), [pt], [A["WG"]], E=E)
            for s in range(8):
                qr_, qi_ = QW[:, 0, s, :], QW[:, 1, s, :]
                dve_tt(tm1[:], Bbs[:], bc(qr_), ALU.mult, [Bbs, QW], [tm1])
                dve_tt(tm2[:], Bbr[:], bc(qi_), ALU.mult, [Bbr, QW], [tm2])
                dve_tt(scrA[:, :, s, :], tm1[:], tm2[:], ALU.add, [tm1, tm2], [scrA])
            craw = sbi([128, 4, 256], F32, "craw")
            cre = self.c_re[l].rearrange("(b g) c p -> (g c) b p", b=4)
            cim = self.c_im[l].rearrange("(b g) c p -> (g c) b p", b=4)
            self.dma("sp", craw[:, :, 0:64], cre, [self.c_re], [craw])
            self.dma("sp", craw[:, :, 64:128], cim, [self.c_im], [craw])
            self.dma("sp", craw[:, :, 128:192], cim, [self.c_im], [craw])
            self.dma("sp", craw[:, :, 192:256], cre, [self.c_re], [craw])
            T1 = sbi([128, G, 16], F32, "T1")
            T2 = sbi([128, G, 16], F32, "T2")
            pt = self.ps[4]
            pt2 = self.ps[5]
            for b in range(4):
                self.tr(pt[:, b * 128:(b + 1) * 128], craw[:, b, 0:128], self.idf[:, :], [craw, self.idf], [pt])
                self.tr(pt2[:, b * 128:(b + 1) * 128], craw[:, b, 128:256], self.idf[:, :], [craw, self.idf], [pt2])
            self.cp(T1[:].rearrange("p g c -> p (g c)"), pt[:, :], [pt], [T1])
            self.ts(T1[64:128, :, :], T1[64:128, :, :], -1.0, None, ALU.mult, None, [T1], [T1])
            self.ts(T2[:].rearrange("p g c -> p (g c)"), pt2[:, :], -1.0, None, ALU.mult, None, [pt2], [T2])
            for t in range(9):
                pr_, pi_ = PW[:, 0, t, :], PW[:, 1, t, :]
                dve_tt(tm1[:], T1[:], bc(pr_), ALU.mult, [T1, PW], [tm1])
                dve_tt(tm2[:], T2[:], bc(pi_), ALU.mult, [T2, PW], [tm2])
                if t < 8:
                    dve_tt(scrB[:, :, t, :], tm1[:], tm2[:], ALU.add, [tm1, tm2], [scrB])
                if t >= 1:
                    for par in range(2):
                        dve_tt(A["WY"][:, par:G:2, t - 1, par * 16:par * 16 + 16], tm1[:, par:G:2, :], tm2[:, par:G:2, :],
                               ALU.add, [tm1, tm2], [A["WY"]])
                        dve_tt(A["WYb"][:, par:8:2, t - 1, 32 + par * 16:32 + par * 16 + 16], tm1[:, 6 + par:G:8, :],
                               tm2[:, 6 + par:G:8, :], ALU.add, [tm1, tm2], [A["WYb"]])
            self.dbg("fr", fr, fr[:], [128, G])
            self.dbg("fi", fi, fi[:], [128, G])
            self.dbg("Bbs", Bbs, Bbs[:], [128, G, 16])
            self.dbg("T1", T1, T1[:], [128, G, 16])
            self.dbg("T2", T2, T2[:], [128, G, 16])
            self.dbg("WG", A["WG"], A["WG"][:], [128, 2, G, 128], BF16)
            self.dbg("WY", A["WY"], A["WY"][:], [128, G, 8, 32], BF16)
            dm = sbi([16, G], F32, "dm")
            self.dma("sp", dm[:], self.ssm_d[l].rearrange("(g c) -> c g", c=16), [self.ssm_d], [dm],
                     allow_slow_non_contiguous=True)
            dcol = sbi([128, G], F32, "dcol")
            pt = self.ps[6]
            self.mm(pt[:, 0:G], self.rep[:, :], dm[:, :], True, True, [self.rep, dm], [pt])
            self.cp(dcol[:], pt[:, 0:G], [pt], [dcol])
            w0t = sbi([128, 4, 128], F32, "w0t")
            for g0 in range(0, G, 4):
                pt = self.ps[(g0 // 4) % 4]
                for j in range(4):
                    g = g0 + j
                    self.mm(pt[:, j * 128:(j + 1) * 128], scrA[:, g, :, :].rearrange("p s c -> p (s c)"), scrB[:, g, :, :].rearrange("p s c -> p (s c)"), True, True, [scrA, scrB], [pt])
                self.tt(w0t[:], pt[:, :].rearrange("p (g f) -> p g f", g=4),
                        self.mst[:, :].unsqueeze(1).broadcast_to([128, 4, 128]), ALU.mult, [pt, self.mst], [w0t])
                for j in range(4):
                    g = g0 + j
                    off = (g % 2) * 16
                    if g % 8 >= 6:
                        gi = (g // 8) * 2 + (g % 2)
                        wdst, wt = A["W0b"][:, gi, :, 32 + off:32 + off + 16], A["W0b"]
                    else:
                        wdst, wt = A["W0"][:, g, :, off:off + 16], A["W0"]
                    self.stt(wdst, self.idf[:, :].rearrange("p (t c) -> p t c", c=16),
                             dcol[:, g:g + 1], w0t[:, j, :].rearrange("p (t c) -> p t c", c=16), ALU.mult, ALU.add,
                             [self.idf, dcol, w0t], [wt])

    def passA(self, st, sq, l):
        A = self.A
        self.dbg("W0", A["W0"], A["W0"][:], [128, G, 8, 32], BF16)
        self.dbg("W0b", A["W0b"], A["W0b"][:], [128, 8, 8, 64], BF16)
        Ttot, MT, r = sq["T"], sq["MT"], sq["r"]
        b = sq["b"]
        nm = MT // 8
        nsub = (MT + 127) // 128
        nmt = Ttot // MT
        xsrc, xb = self.x_src(sq, l)
        H = A["H"]
        RE = self.pool
        if b is None:
            self.ms(H[:, 0, :], 0.0, [H], E=RE)
        else:
            st32 = A["st32"]
            self.dma("sp", st32[:, 0:64], self.sre[l, b], [self.sre], [st32])
            self.dma("sp", st32[:, 64:128], self.sim[l, b], [self.sim], [st32])
            self.dma("sp", st32[:, 128:192], self.sim[l, b], [self.sim], [st32])
            self.dma("sp", st32[:, 192:256], self.sre[l, b], [self.sre], [st32])
            pt = self.ps[7]
            self.tr(pt[:, 0:32], st32[:, 0:128], self.idf[0:32, 0:32], [st32, self.idf], [pt])
            self.tr(pt[:, 32:64], st32[:, 128:256], self.idf[0:32, 0:32], [st32, self.idf], [pt])
            self.cp(H[:, 0, :], pt[:, 0:64], [pt], [H])
            self.ts(H[0:64, 0, 32:64], H[0:64, 0, 32:64], -1.0, None, ALU.mult, None, [H], [H])

        def bufs(mt):
            big = A["big"][mt % 2]
            return big, big[:, :].rearrange("p (m v g) -> p m v g", v=2, g=G), A["U8"][mt % 2], A["zs"][mt % 2]

        def X(mt):
            tok0 = mt * MT
            hT = A["hT"]
            big, Gsv, U8, zs = bufs(mt)
            for s_ in range(nsub):
                n = min(128, MT - s_ * 128)
                self.norm_tile(self.x_rows(xsrc, xb, tok0 + s_ * 128, n), xsrc, n, r, A["xt"], A["xn"], hT, s_ * 128,
                               A["small"], self.ps[0], all_act=True)
            XX = A["XX"]
            for s in range(8):
                pt = self.ps[1 + (s % 2)]
                for dc in range(DC):
                    self.mm(pt[0:nm, :], hT[:, dc, s:MT:8], A["w_in"][:, dc, 0:512], dc == 0, dc == DC - 1,
                            [hT, A["w_in"]], [pt])
                self.cp(XX[0:nm, :, s, :], pt[0:nm, :].rearrange("p (g c) -> p g c", c=16), [pt], [XX],
                        E=self.act)
            for g0 in range(0, G, 8):
                pt = self.ps[1 + (g0 // 8) % 2]
                pb = pt[:].bitcast(BF16)
                for j in range(8):
                    g = g0 + j
                    self.tr(pb[:, j * 64:j * 64 + nm], XX[0:nm, g, :, :].rearrange("p s c -> p (s c)"), self.idb[0:nm, 0:nm],
                            [XX, self.idb], [pt])
                self.cp(U8[:, g0:g0 + 8, 0:nm], pb[:, 0:512].rearrange("p (g m) -> p g m", g=8)[:, :, 0:nm], [pt], [U8])
            for fc in range(4):
                pt = self.ps[1 + (fc % 2)]
                for dc in range(DC):
                    self.mm(pt[:, 0:MT], A["w_in"][:, dc, 512 + fc * 128:512 + (fc + 1) * 128], hT[:, dc, 0:MT], dc == 0,
                            dc == DC - 1, [A["w_in"], hT], [pt])
                self.a(zs[:, fc, 0:MT], pt[:, 0:MT], AF.Silu, [pt], [zs])
            for g0 in range(0, G, 4):
                pt = self.ps[1 + (g0 // 4) % 2]
                for vi in range(2):
                    for j in range(4):
                        g = g0 + j
                        c0 = (vi * 4 + j) * 64
                        self.mm(pt[:, c0:c0 + nm], A["WG"][:, vi, g, :], U8[:, g, 0:nm], True, True, [A["WG"], U8], [pt])
                for vi in range(2):
                    self.cp(Gsv[:, 0:nm, vi, g0:g0 + 4].rearrange("p m g -> p g m"),
                            pt[:, vi * 256:(vi + 1) * 256].rearrange("p (g m) -> p g m", g=4)[:, :, 0:nm], [pt], [big],
                            E=self.act)

        def capply(E, dst, src, addend, CRt, CIt, kq, ta, tb, rd, wr):
            crb = CRt[:, :].unsqueeze(1).broadcast_to([128, kq, 64])
            cia = CIt[:, 0:32].unsqueeze(1).broadcast_to([128, kq, 32])
            cib = CIt[:, 32:64].unsqueeze(1).broadcast_to([128, kq, 32])
            self.tt(ta[:, 0:kq, :], src, crb, ALU.mult, rd + [CRt], [ta], E=E)
            self.tt(tb[:, 0:kq, 0:32], src[:, :, 32:64], cia, ALU.mult, rd + [CIt], [tb], E=E)
            self.tt(tb[:, 0:kq, 32:64], src[:, :, 0:32], cib, ALU.mult, rd + [CIt], [tb], E=E)
            self.tt(ta[:, 0:kq, :], ta[:, 0:kq, :], tb[:, 0:kq, :], ALU.add, [ta, tb], [ta], E=E)
            self.tt(dst, ta[:, 0:kq, :], addend, ALU.add, [ta] + rd, wr, E=E)

        def B2(mt):
            big, Gsv, U8, zs = bufs(mt)
            Gf = Gsv.rearrange("p m v g -> p m (v g)")
            nk2 = nm // 2
            for k0 in range(0, nk2, 8):
                kq = min(8, nk2 - k0)
                ev = Gf[:, 2 * k0:2 * (k0 + kq):2, :]
                od = Gf[:, 2 * k0 + 1:2 * (k0 + kq):2, :]
                capply(self.dve, od, ev, od, A["CR"], A["CI"], kq, A["q1"], A["q2"], [big], [big])

        def Rc(mt):
            big, Gsv, U8, zs = bufs(mt)
            Gf = Gsv.rearrange("p m v g -> p m (v g)")
            t1, t2 = A["t1"], A["t2"]
            nk2 = nm // 2
            if mt > 0:
                self.cp(H[:, 0, :], H[:, nm, :], [H], [H], E=RE)
            for k in range(nk2):
                m = 2 * k
                self.tt(t1[:], H[:, m, :], A["CR2"][:], ALU.mult, [H, A["CR2"]], [t1], E=RE)
                self.tt(t2[:, 0:32], H[:, m, 32:64], A["CI2"][:, 0:32], ALU.mult, [H, A["CI2"]], [t2], E=RE)
                self.tt(t2[:, 32:64], H[:, m, 0:32], A["CI2"][:, 32:64], ALU.mult, [H, A["CI2"]], [t2], E=RE)
                self.tt(t1[:], t1[:], t2[:], ALU.add, [t1, t2], [t1], E=RE)
                self.tt(H[:, m + 2, :], t1[:], Gf[:, m + 1, :], ALU.add, [t1, big], [H], E=RE)
            for k0 in range(0, nk2, 8):
                kq = min(8, nk2 - k0)
                hev = H[:, 2 * k0:2 * (k0 + kq):2, :]
                hod = H[:, 2 * k0 + 1:2 * (k0 + kq):2, :]
                gev = Gf[:, 2 * k0:2 * (k0 + kq):2, :]
                capply(RE, hod, hev, gev, A["CR"], A["CI"], kq, A["q3"], A["q4"], [H, big], [H])
            self.cp(A["S0"][:, :, 0:nm], H[:, 0:nm, 0:32].rearrange("p m g -> p g m"), [H], [A["S0"]], E=RE)

        def Y(mt):
            tok0 = mt * MT
            big, Gsv, U8, zs = bufs(mt)
            S0 = A["S0"]
            for fc in range(4):
                pt = self.ps[3 + fc]
                for t in range(8):
                    for pp in range(2):
                        o = pt[32 * pp:32 * pp + 32, t:MT:8]
                        for gi, g in enumerate((8 * fc + 2 * pp, 8 * fc + 2 * pp + 1)):
                            self.mm(o, A["W0"][:, g, t, :], U8[:, g, 0:nm], gi == 0, False, [A["W0"], U8], [pt])
                            self.mm(o, A["WY"][:, g, t, :], S0[:, g, 0:nm], False, gi == 1, [A["WY"], S0], [pt])
                    o64 = pt[64:128, t:MT:8]
                    o32 = pt[64:96, t:MT:8]
                    for gi in range(2):
                        g = 8 * fc + 6 + gi
                        self.mm(o64, A["W0b"][:, 2 * fc + gi, t, :], U8[:, g, 0:nm], gi == 0, False, [A["W0b"], U8], [pt])
                        self.mm(o64, A["WYb"][:, 2 * fc + gi, t, :], S0[:, g, 0:nm], False, False, [A["WYb"], S0], [pt])
                    for gi in range(2):
                        g = 8 * fc + 4 + gi
                        self.mm(o32, A["W0"][:, g, t, :], U8[:, g, 0:nm], False, False, [A["W0"], U8], [pt])
                        self.mm(o32, A["WY"][:, g, t, :], S0[:, g, 0:nm], False, gi == 1, [A["WY"], S0], [pt])
            gs = A["gs"]
            mo = gs
            yy = big[:, 0:2048].rearrange("p (c t) -> p c t", c=4)
            uu = big[:, 2048:4096].rearrange("p (c t) -> p c t", c=4)
            for fc in range(4):
                self.cp(yy[:, fc, 0:MT], self.ps[3 + fc][:, 0:MT], [self.ps[3 + fc]], [big], E=self.act)
            yv, uv = yy[:, :, 0:MT], uu[:, :, 0:MT]
            self.tt(uv, yv, yv, ALU.mult, [big], [big])
            self.ts(uv, uv, 0.044715, 1.0, ALU.mult, ALU.add, [big], [big])
            self.tt(uv, uv, yv, ALU.mult, [big], [big])
            self.a(uv, uv, AF.Sigmoid, [big], [big], scale=2.0 * math.sqrt(2.0 / math.pi))
            self.tt(yv, yv, uv, ALU.mult, [big], [big])
            self.cp(gs[:, :, 0:MT], yv, [big], [gs])
            for fc in range(4):
                pt = self.ps[1 + fc % 2]
                for kc in range(4):
                    self.mm(pt[:, 0:MT], A["w_glu"][:, kc, fc * 128:(fc + 1) * 128], gs[:, kc, 0:MT], kc == 0, kc == 3,
                            [A["w_glu"], gs], [pt])
                self.a(uu[:, fc, 0:MT], pt[:, 0:MT], AF.Sigmoid, [pt, A["bglu"]], [big], bias=A["bglu"][:, fc:fc + 1])
            self.tt(yv, yv, uv, ALU.mult, [big], [big])
            self.tt(mo[:, :, 0:MT], yv, zs[:, :, 0:MT], ALU.mult, [big, zs], [mo])
            if b is None:
                self.dma("sp", self.mixs[:, tok0:tok0 + MT].rearrange("(c p) t -> p c t", p=128), mo[:, :, 0:MT],
                         [mo], [self.mixs])
            else:
                self.dma("sp", self.mixss[b].rearrange("(c p) t -> p c t", p=128), mo[:, :, 0:MT], [mo], [self.mixss])

        X(0)
        B2(0)
        Rc(0)
        for mt in range(nmt):
            if mt + 1 < nmt:
                X(mt + 1)
                B2(mt + 1)
            Y(mt)
            if mt + 1 < nmt:
                Rc(mt + 1)
        fin = A["fin"]
        pt = self.ps[7]
        fsrc = A["t1"]
        self.cp(fsrc[:, 0:32], H[:, nm, 0:32], [H], [fsrc], E=RE)
        self.tr(pt[0:32, 0:128], fsrc[:, 0:32], self.idf[:, :], [fsrc, self.idf], [pt])
        self.cp(fin[:, :], pt[0:32, 0:128], [pt], [fin])
        if b is None:
            self.dma("sp", self.pr[l], fin[:, 0:64], [fin], [self.pr])
            self.dma("sp", self.pi[l], fin[:, 64:128], [fin], [self.pi])
        else:
            self.dma("sp", self.sr[l, b], fin[:, 0:64], [fin], [self.sr])
            self.dma("sp", self.si[l, b], fin[:, 64:128], [fin], [self.si])

    def passB_setup(self, st, l):
        sbi = lambda shape, dt, name: self.sb_in(st, shape, dt, name)
        B = {}
        self.B = B
        B["w_in"] = sbi([128, DC, 2048], BF16, "w_inB")
        for h4 in range(4):
            self.dma("pool", B["w_in"][:, h4 * 2:(h4 + 1) * 2, :],
                     self.w_in[l, h4 * 256:(h4 + 1) * 256, 1024:3072].rearrange("(c p) f -> p c f", p=128),
                     [self.w_in], [B["w_in"]])
        B["w_out"] = sbi([128, DC, D], BF16, "w_out")
        for h2 in range(2):
            self.dma("pool", B["w_out"][:, h2 * 4:(h2 + 1) * 4, :],
                     self.w_out[l, h2 * 512:(h2 + 1) * 512, :].rearrange("(c p) f -> p c f", p=128),
                     [self.w_out], [B["w_out"]])
        KW = max(self.SEQ, self.PAST + self.TS)
        NVB = max(self.SEQ // 128, self.PAST // 128 + 1)
        B["KT"] = sbi([128, 4, KW], BF16, "KT")
        B["V"] = sbi([128, NVB, ATT], BF16, "Vres")
        B["gq"] = sbi([128, HD], F32, "gq")
        B["gk"] = sbi([128, HD], F32, "gk")
        self.dma("sp", B["gq"][:], self.qg[l:l + 1, :].broadcast_to([128, HD]), [self.qg], [B["gq"]])
        self.dma("sp", B["gk"][:], self.kg[l:l + 1, :].broadcast_to([128, HD]), [self.kg], [B["gk"]])
        self.ts(B["gq"][:], B["gq"][:], HD ** -0.5, None, ALU.mult, None, [B["gq"]], [B["gq"]])
        B["xt"] = sbi([128, D], F32, "xtB")
        B["xr"] = sbi([128, D], F32, "xrB")
        B["ot"] = [sbi([128, 512], F32, "ot0"), sbi([128, 512], F32, "ot1")]
        B["xn"] = sbi([128, D], BF16, "xnB")
        B["hT"] = sbi([128, DC, 512], BF16, "hTB")
        B["small"] = self.mk_small(st)
        B["sqk"] = sbi([128, ATT], F32, "sqk")
        B["ssq"] = sbi([128, 16], F32, "ssq")
        B["qf"] = sbi([128, ATT], F32, "qf")
        B["kf"] = sbi([128, ATT], F32, "kf")
        B["vf"] = sbi([128, ATT], F32, "vf")
        B["qb"] = sbi([128, ATT], BF16, "qb")
        B["kb"] = sbi([128, ATT], BF16, "kb")
        B["QT"] = [sbi([128, 4, 512], BF16, "QT0"), sbi([128, 4, 512], BF16, "QT1")]
        B["za"] = [sbi([128, 4, 512], BF16, "za0"), sbi([128, 4, 512], BF16, "za1")]
        B["mix"] = [sbi([128, DC, 512], BF16, "mixT0"), sbi([128, DC, 512], BF16, "mixT1")]
        B["e"] = [sbi([128, 512], F32, "e0"), sbi([128, 512], F32, "e1")]
        B["sp"] = [sbi([128, 512], BF16, "sp0"), sbi([128, 512], BF16, "sp1")]
        B["at"] = [sbi([128, 512], BF16, "at0"), sbi([128, 512], BF16, "at1")]
        B["R"] = [sbi([128, 512], BF16, "R0"), sbi([128, 512], BF16, "R1")]

    def treduce(self, out, in_, reads, writes):
        self.op(self.dve, lambda: self.nc.vector.tensor_reduce(out=out, in_=in_, axis=AX.X, op=ALU.add), reads, writes)

    def passB(self, st, sq, l):
        B = self.B
        Ttot, MT, r, b, past = sq["T"], sq["MT"], sq["r"], sq["b"], sq["past"]
        nsub = (MT + 127) // 128
        nmt = Ttot // MT
        xsrc, xb = self.x_src(sq, l)
        xdst = self.yp if b is None else self.ys
        KT, V = B["KT"], B["V"]
        kvb = [Buf() for _ in range(nmt)]
        pastb = Buf()
        if past:
            npb = past // 128
            self.dma("pool", V[:, 0:npb, :], self.cv[l, b].rearrange("(n p) f -> p n f", p=128), [self.cv], [pastb])
            for n0 in range(npb):
                kb = B["kb"]
                self.dma("pool", kb[:, :], self.ck[l, b, n0 * 128:(n0 + 1) * 128, :], [self.ck], [kb])
                pt = self.ps[n0 % 2]
                pb = pt[:].bitcast(BF16)
                for c in range(4):
                    self.tr(pb[:, c * 128:(c + 1) * 128], kb[:, c * 128:(c + 1) * 128], self.idb[:, :], [kb, self.idb], [pt])
                self.cp(KT[:, :, n0 * 128:(n0 + 1) * 128], pb[:, 0:512].rearrange("p (c t) -> p c t", c=4), [pt], [pastb],
                        E=(self.act if n0 % 2 else self.dve))

        def fe(mt):
            tok0 = mt * MT
            hT = B["hT"]
            QT, za = B["QT"][mt % 2], B["za"][mt % 2]
            for s_ in range(nsub):
                n = min(128, MT - s_ * 128)
                self.norm_tile(self.x_rows(xsrc, xb, tok0 + s_ * 128, n), xsrc, n, r, B["xt"], B["xn"], hT, s_ * 128,
                               B["small"], self.ps[0], light=True)
            for s_ in range(nsub):
                n = min(128, MT - s_ * 128)
                g0 = past + tok0 + s_ * 128
                pq, pk_, pv_ = self.ps[0], self.ps[1], self.ps[0]
                sqk, ssq = B["sqk"], B["ssq"]
                for which, pt, gt, of, ob in ((0, pq, B["gq"], B["qf"], B["qb"]), (1, pk_, B["gk"], B["kf"], B["kb"])):
                    for dc in range(DC):
                        self.mm(pt[0:n, :], hT[:, dc, s_ * 128:s_ * 128 + n],
                                B["w_in"][:, dc, which * 512:(which + 1) * 512], dc == 0, dc == DC - 1, [hT, B["w_in"]], [pt])
                    c0 = which * 8
                    self.a(sqk[0:n, :], pt[0:n, :], AF.Square, [pt], [sqk])
                    self.treduce(ssq[0:n, c0:c0 + 8], sqk[0:n, :].rearrange("p (h d) -> p h d", d=HD), [sqk], [ssq])
                    self.a(ssq[0:n, c0:c0 + 8], ssq[0:n, c0:c0 + 8], AF.Ln, [ssq, B["small"]["eps"]], [ssq],
                           scale=1.0 / HD, bias=B["small"]["eps"][0:n, 0:1])
                    self.a(ssq[0:n, c0:c0 + 8], ssq[0:n, c0:c0 + 8], AF.Exp, [ssq], [ssq], scale=-0.5)
                    self.tt(of[0:n, :].rearrange("p (h d) -> p h d", d=HD), pt[0:n, :].rearrange("p (h d) -> p h d", d=HD),
                            ssq[0:n, c0:c0 + 8].unsqueeze(2).broadcast_to([n, NH, HD]), ALU.mult, [pt, ssq], [of])
                    if which == 0:
                        self.tt(ob[0:n, :].rearrange("p (h d) -> p h d", d=HD), of[0:n, :].rearrange("p (h d) -> p h d", d=HD),
                                gt[0:n, :].unsqueeze(1).broadcast_to([n, NH, HD]), ALU.mult, [of, gt], [ob])
                    else:
                        self.tt(of[0:n, :].rearrange("p (h d) -> p h d", d=HD), of[0:n, :].rearrange("p (h d) -> p h d", d=HD),
                                gt[0:n, :].unsqueeze(1).broadcast_to([n, NH, HD]), ALU.mult, [of, gt], [of])
                        self.cp(ob[0:n, :], of[0:n, :], [of], [ob])
                        dst = self.pk[l, tok0 + s_ * 128:tok0 + s_ * 128 + n, :] if b is None else self.sk[l, b, :, :]
                        self.dma("sp", dst, of[0:n, :], [of], [self.pk if b is None else self.sk])
                for dc in range(DC):
                    self.mm(pv_[0:n, :], hT[:, dc, s_ * 128:s_ * 128 + n], B["w_in"][:, dc, 1024:1536], dc == 0,
                            dc == DC - 1, [hT, B["w_in"]], [pv_])
                vf = B["vf"]
                self.cp(vf[0:n, :], pv_[0:n, :], [pv_], [vf])
                dst = self.pv[l, tok0 + s_ * 128:tok0 + s_ * 128 + n, :] if b is None else self.sv[l, b, :, :]
                self.dma("sp", dst, vf[0:n, :], [vf], [self.pv if b is None else self.sv])
                self.cp(V[0:n, g0 // 128, :], vf[0:n, :], [vf], [kvb[mt]], E=self.pool)
                ptq, ptk = self.ps[1], self.ps[0]
                pbq, pbk = ptq[:].bitcast(BF16), ptk[:].bitcast(BF16)
                for c in range(4):
                    self.tr(pbq[:, c * 128:c * 128 + n], B["qb"][0:n, c * 128:(c + 1) * 128], self.idb[0:n, 0:n],
                            [B["qb"], self.idb], [ptq])
                    self.tr(pbk[:, c * 128:c * 128 + n], B["kb"][0:n, c * 128:(c + 1) * 128], self.idb[0:n, 0:n],
                            [B["kb"], self.idb], [ptk])
                self.cp(QT[:, :, s_ * 128:s_ * 128 + n], pbq[:, 0:512].rearrange("p (c t) -> p c t", c=4)[:, :, 0:n],
                        [ptq], [QT])
                self.cp(KT[:, :, g0:g0 + n], pbk[:, 0:512].rearrange("p (c t) -> p c t", c=4)[:, :, 0:n], [ptk], [kvb[mt]])
            for fc in range(4):
                pt = self.ps[fc % 2]
                for dc in range(DC):
                    self.mm(pt[:, 0:MT], B["w_in"][:, dc, 1536 + fc * 128:1536 + (fc + 1) * 128], hT[:, dc, 0:MT], dc == 0,
                            dc == DC - 1, [B["w_in"], hT], [pt])
                self.a(za[:, fc, 0:MT], pt[:, 0:MT], AF.Silu, [pt], [za])

        def kv_of(k0):
            return pastb if k0 < past else kvb[(k0 - past) // MT]

        def att(mt, extra):
            tok0 = mt * MT
            QT, za = B["QT"][mt % 2], B["za"][mt % 2]
            mix = B["mix"][mt % 2]
            blocks = []
            for s_ in reversed(range(nsub)):
                n = min(128, MT - s_ * 128)
                blocks.append((past + tok0 + s_ * 128, n, s_ * 128, True))
            for k0 in reversed(range(0, past + tok0, 128)):
                blocks.append((k0, 128, 0, False))
            units = []
            nb = len(blocks)
            for pr_ in range(NH // 2):
                for bi, blk in enumerate(blocks):
                    for hh in range(2):
                        units.append((2 * pr_ + hh, bi) + blk)
            nu = len(units)

            def slot(i):
                return self.ps[2 + (i % 5)], B["e"][i % 2], B["sp"][i % 2], B["at"][i % 2]

            def S1(i):
                h, bi, k0, nk, cs_, diag = units[i]
                c, po = h // 2, 64 * (h % 2)
                sc = slot(i)[0]
                o = sc[0:nk, cs_:MT]
                self.mm(o, KT[po:po + 64, c, k0:k0 + nk], QT[po:po + 64, c, cs_:MT], True, not diag, [kv_of(k0), QT], [sc])
                if diag:
                    self.mm(sc[0:nk, cs_:cs_ + nk], self.idb[0:nk, 0:nk], self.negm[0:nk, 0:nk], False, True,
                            [self.idb, self.negm], [sc])

            def S2a(i):
                h, bi, k0, nk, cs_, diag = units[i]
                sc, e, sp_, at = slot(i)
                if bi == 0:
                    self.ms(B["R"][h % 2][:, 0:MT], 0.0, [B["R"][h % 2]], E=self.pool)
                self.a(e[0:nk, cs_:MT], sc[0:nk, cs_:MT], AF.Exp, [sc], [e])

            def S2b(i):
                h, bi, k0, nk, cs_, diag = units[i]
                sc, e, sp_, at = slot(i)
                self.a(sp_[0:nk, cs_:MT], e[0:nk, cs_:MT], AF.Ln, [e], [sp_], bias=1.0)

            def S3(i):
                h, bi, k0, nk, cs_, diag = units[i]
                sc, e, sp_, at = slot(i)
                R = B["R"][h % 2]
                o = sc[0:nk, cs_:MT]
                self.mm(o, self.ntri[0:nk, 0:nk], sp_[0:nk, cs_:MT], False, False, [self.ntri, sp_], [sc])
                if bi > 0:
                    self.mm(o, self.nones[0:128, 0:nk], R[0:128, cs_:MT], False, True, [self.nones, R], [sc])
                if bi < nb - 1:
                    self.tt(R[0:nk, cs_:MT], R[0:nk, cs_:MT], sp_[0:nk, cs_:MT], ALU.add, [R, sp_], [R])

            def S4(i):
                h, bi, k0, nk, cs_, diag = units[i]
                sc, e, sp_, at = slot(i)
                self.a(at[0:nk, cs_:MT], sc[0:nk, cs_:MT], AF.Exp, [sc], [at])

            def S5(i):
                h, bi, k0, nk, cs_, diag = units[i]
                c, po = h // 2, 64 * (h % 2)
                sc, e, sp_, at = slot(i)
                oacc = self.ps[7]
                self.mm(oacc[po:po + 64, cs_:MT], V[0:nk, k0 // 128, h * HD:(h + 1) * HD], at[0:nk, cs_:MT], bi == 0,
                        bi == nb - 1, [kv_of(k0), at], [oacc])
                if bi == nb - 1:
                    self.tt(mix[po:po + 64, 4 + c, 0:MT], oacc[po:po + 64, 0:MT], za[po:po + 64, c, 0:MT], ALU.mult,
                            [oacc, za], [mix])

            ne = len(extra)
            done = 0
            S1(0)
            for i in range(nu + 2):
                if i % 2 == 0:
                    if i + 1 < nu:
                        S1(i + 1)
                    if i + 2 < nu:
                        S1(i + 2)
                if i < nu:
                    S2a(i)
                if 0 <= i - 2 < nu:
                    S4(i - 2)
                if i < nu:
                    S2b(i)
                if 0 <= i - 1 < nu:
                    S3(i - 1)
                if 0 <= i - 2 < nu:
                    S5(i - 2)
                tgt = (ne * (i + 1)) // max(1, nu - 2) if nu > 2 else ne
                tgt = min(ne, tgt)
                while done < tgt:
                    extra[done]()
                    done += 1
            while done < ne:
                extra[done]()
                done += 1

        def outp(mt):
            tok0 = mt * MT
            mix = B["mix"][mt % 2]
            if b is None:
                self.dma("sp", mix[:, 0:4, 0:MT], self.mixs[:, tok0:tok0 + MT].rearrange("(c p) t -> p c t", p=128),
                         [self.mixs], [mix])
            else:
                self.dma("sp", mix[:, 0:4, 0:MT], self.mixss[b].rearrange("(c p) t -> p c t", p=128), [self.mixss], [mix])
            for s_ in range(nsub):
                n = min(128, MT - s_ * 128)
                xr = B["xr"]
                self.dma("sp", xr[0:n, :], self.x_rows(xsrc, xb, tok0 + s_ * 128, n), [xsrc], [xr])
                for hf in range(2):
                    pt = self.ps[hf]
                    ot = B["ot"][hf]
                    for kc in range(DC):
                        self.mm(pt[0:n, :], mix[:, kc, s_ * 128:s_ * 128 + n], B["w_out"][:, kc, hf * 512:(hf + 1) * 512],
                                kc == 0, kc == DC - 1, [mix, B["w_out"]], [pt])
                    self.tt(ot[0:n, :], pt[0:n, :], self.gate_bc[0:n, r, hf * 512:(hf + 1) * 512],
                            ALU.mult, [pt, self.gate_bc], [ot])
                    self.tt(ot[0:n, :], ot[0:n, :], xr[0:n, hf * 512:(hf + 1) * 512], ALU.add, [ot, xr], [ot], E=self.pool)
                    self.dma("sp", self.x_rows(xdst, xb, tok0 + s_ * 128, n)[:, hf * 512:(hf + 1) * 512], ot[0:n, :],
                             [ot], [xdst])

        fe(0)
        for mt in range(nmt):
            extra = self.record(lambda: outp(mt - 1)) if mt >= 1 else []
            if mt + 1 < nmt:
                extra = extra + self.record(lambda: fe(mt + 1))
            att(mt, extra)
        outp(nmt - 1)


def host_consts():
    bf = ml_dtypes.bfloat16
    idx = np.arange(128)
    c = {}
    c["c_idb"] = np.eye(128, dtype=np.float32).astype(bf)
    c["c_idf"] = np.eye(128, dtype=np.float32)
    c["c_ntri"] = (-(idx[:, None] >= idx[None, :]).astype(np.float32)).astype(bf)
    c["c_nones"] = (-np.ones((128, 128), np.float32)).astype(bf)
    c["c_negm"] = (NEG * (idx[:, None] >= idx[None, :]).astype(np.float32)).astype(bf)
    c["c_mst"] = ((idx[:, None] // 16) <= (idx[None, :] // 16)).astype(np.float32)
    c["c_rep"] = np.tile(np.eye(16, dtype=np.float32), (1, 8))
    return c


_NC_CACHE = {}


def run(inputs, SEQ, DEPTH, NB=8, NS=2, debug=False):
    TS = inputs["x_sample"].shape[1]
    PAST = inputs["cache_k"].shape[2]
    key = (SEQ, DEPTH, NS, TS, PAST, debug)
    if key not in _NC_CACHE:
        kk = Kern(SEQ=SEQ, DEPTH=DEPTH, NS=NS, TS=TS, PAST=PAST)
        kk.debug = debug
        _NC_CACHE[key] = (kk.build(), kk.dbg_names)
    nc, dbg_names = _NC_CACHE[key]
    f = lambda a: np.ascontiguousarray(np.asarray(a, dtype=np.float32))
    consts = host_consts()
    L = DEPTH
    wmap = {
        "norm_g": f(inputs["norm_g"]), "w_mod": f(inputs["w_mod"]), "b_mod": f(inputs["b_mod"]), "w_in": f(inputs["w_in"]),
        "a_re": f(inputs["ssm_a_re"]), "a_im": f(inputs["ssm_a_im"]), "log_dt": f(inputs["ssm_log_dt"]),
        "b_re": f(inputs["ssm_b_re"]), "b_im": f(inputs["ssm_b_im"]), "c_re": f(inputs["ssm_c_re"]),
        "c_im": f(inputs["ssm_c_im"]), "ssm_d": f(inputs["ssm_d"]), "w_glu": f(inputs["w_glu"]), "b_glu": f(inputs["b_glu"]),
        "qg": f(inputs["q_norm_g"]), "kg": f(inputs["k_norm_g"]), "w_out": f(inputs["w_out"]),
    }
    xp, xs = f(inputs["x_prompt"]), f(inputs["x_sample"])
    cp_, cs_ = f(inputs["c_prompt"]), f(inputs["c_sample"])
    ck, cv = inputs["cache_k"], inputs["cache_v"]
    sre, sim = f(inputs["state_ssm_re"]), f(inputs["state_ssm_im"])
    in_maps = []
    for i in range(NB):
        sl = slice(i * NS, (i + 1) * NS)
        m = dict(wmap)
        m.update(consts)
        m["xp"] = xp[i]
        m["xs"] = xs[sl]
        m["cc"] = np.concatenate([cp_[i:i + 1], cs_[sl]], axis=0)
        m["ck"] = f(ck[:, sl]).reshape(L, NS, PAST, ATT)
        m["cv"] = f(cv[:, sl]).reshape(L, NS, PAST, ATT)
        m["sre"] = sre[:, sl]
        m["sim"] = sim[:, sl]
        in_maps.append({k: np.ascontiguousarray(v) for k, v in m.items()})
    res = run_bass_kernel_spmd(nc, in_maps, core_ids=list(range(NB)))
    R = res.results
    if debug:
        run.dbg = {n: np.asarray(R[0]["dbg_" + n]).astype(np.float32) for n in dbg_names}
    yp = np.stack([R[i]["yp"] for i in range(NB)], 0)
    ys = np.concatenate([R[i]["ys"] for i in range(NB)], 0)
    pk = np.stack([R[i]["pk"] for i in range(NB)], 1).reshape(L, NB, SEQ, NH, HD)
    pv = np.stack([R[i]["pv"] for i in range(NB)], 1).reshape(L, NB, SEQ, NH, HD)
    pr = np.stack([R[i]["pr"] for i in range(NB)], 1)
    pi = np.stack([R[i]["pi"] for i in range(NB)], 1)
    sk = np.concatenate([R[i]["sk"] for i in range(NB)], 1).reshape(L, NB * NS, TS, NH, HD)
    sv = np.concatenate([R[i]["sv"] for i in range(NB)], 1).reshape(L, NB * NS, TS, NH, HD)
    sr = np.concatenate([R[i]["sr"] for i in range(NB)], 1)
    si = np.concatenate([R[i]["si"] for i in range(NB)], 1)
    return tuple(np.asarray(a, dtype=np.float32) for a in (yp, ys, pk, pv, pr, pi, sk, sv, sr, si))


def kernel(**inputs):
    SEQ = inputs["x_prompt"].shape[1]
    DEPTH = inputs["w_in"].shape[0]
    return run(inputs, SEQ, DEPTH)
```
